# Optimizing a Trainium2 kernel written in Bass

```python
import math
import jax, jax.numpy as jnp
from jax import lax
import numpy as np

D_MODEL = 4096
BATCH = 8
SEQ = 2048
DEPTH = 2
DEC_BATCH = 2
DEC_SEQ = 8192
PAST_LEN = 128

N_META = 16
N_MIXERS = 2
RET_HEADS = 16
RET_DK = D_MODEL // RET_HEADS
RET_DV = 2 * RET_DK
RET_QK = RET_HEADS * RET_DK
RET_V = RET_HEADS * RET_DV
RET_CHUNK = 128
RET_THETA = 10000.0
DIFF_HEADS = 32
DIFF_DH = 128
DIFF_DV = 2 * DIFF_DH
DIFF_QK = DIFF_HEADS * 2 * DIFF_DH
DIFF_V = DIFF_HEADS * DIFF_DV
DIFF_ROT = DIFF_DH // 4
ROPE_THETA = 500000.0
Q_BLOCK = 128
N_RET = (DEPTH + 1) // 2
N_DIFF = DEPTH // 2
NORM_EPS = 1e-6

kernel_name = 'hybrid_retention_diffattn_encoder'


def rms_norm(x, gain=None, eps=NORM_EPS):
    xf = x.astype(jnp.float32)
    y = xf * lax.rsqrt(jnp.mean(xf * xf, axis=-1, keepdims=True) + eps)
    if gain is not None:
        y = y * gain.astype(jnp.float32)
    return y.astype(x.dtype)


def rotary(x, pos, rot_dim, theta):
    half = rot_dim // 2
    inv_freq = jnp.power(jnp.float32(theta), -jnp.arange(half, dtype=jnp.float32) * 2.0 / rot_dim)
    ang = pos[:, None] * inv_freq[None, :]
    shape = (pos.shape[0],) + (1,) * (x.ndim - 3) + (half,)
    cos = jnp.cos(ang).reshape(shape)
    sin = jnp.sin(ang).reshape(shape)
    xf = x.astype(jnp.float32)
    x1 = xf[..., :half]
    x2 = xf[..., half:rot_dim]
    out = jnp.concatenate([x1 * cos - x2 * sin, x2 * cos + x1 * sin, xf[..., rot_dim:]], axis=-1)
    return out.astype(x.dtype)


def retention_chunkwise(q, k, v, log_gamma, include_diag):
    B, Lp, H, dk = q.shape
    dv = v.shape[-1]
    C = RET_CHUNK
    N = Lp // C
    qc = q.reshape(B, N, C, H, dk)
    kc = k.reshape(B, N, C, H, dk)
    vc = v.reshape(B, N, C, H, dv)
    idx = jnp.arange(C, dtype=jnp.float32)
    rel = idx[:, None] - idx[None, :]
    mask = (rel >= 0) if include_diag else (rel > 0)
    decay = jnp.where(mask[None], jnp.exp(jnp.where(mask, rel, 0.0)[None] * log_gamma[:, None, None]), 0.0)
    scores = jnp.einsum('bnihd,bnjhd->bnhij', qc, kc) * decay[None, None]
    intra = jnp.einsum('bnhij,bnjhe->bnihe', scores, vc)
    q_decay = jnp.exp((idx + 1.0)[:, None] * log_gamma[None, :])
    k_decay = jnp.exp((C - 1.0 - idx)[:, None] * log_gamma[None, :])
    chunk_decay = jnp.exp(C * log_gamma)

    def step(state, inp):
        qn, kn, vn = inp
        cross = jnp.einsum('bihd,bhde->bihe', qn * q_decay[None, :, :, None], state)
        state = state * chunk_decay[None, :, None, None] + jnp.einsum(
            'bjhd,bjhe->bhde', kn * k_decay[None, :, :, None], vn)
        return state, cross

    init = jnp.zeros((B, H, dk, dv), jnp.float32)
    _, cross = lax.scan(step, init, (jnp.moveaxis(qc, 1, 0), jnp.moveaxis(kc, 1, 0), jnp.moveaxis(vc, 1, 0)))
    out = intra + jnp.moveaxis(cross, 0, 1)
    return out.reshape(B, Lp, H, dv)


def retention_mixer(h, w_in, w_out, decay_fwd_raw, decay_bwd_raw):
    B, L, _ = h.shape
    proj = h @ w_in
    q, k, v, g = jnp.split(proj, [RET_QK, 2 * RET_QK, 2 * RET_QK + RET_V], axis=-1)
    q = q.reshape(B, L, RET_HEADS, RET_DK)
    k = k.reshape(B, L, RET_HEADS, RET_DK)
    v = v.reshape(B, L, RET_HEADS, RET_DV)
    pos = jnp.arange(L, dtype=jnp.float32)
    q = rotary(q, pos, RET_DK, RET_THETA).astype(jnp.float32) * (RET_DK ** -0.5)
    k = rotary(k, pos, RET_DK, RET_THETA).astype(jnp.float32)
    pad = RET_CHUNK - N_META
    padt = lambda t: jnp.pad(t.astype(jnp.float32), ((0, 0), (pad, 0), (0, 0), (0, 0)))
    qp, kp, vp = padt(q), padt(k), padt(v)
    lg_f = -jnp.exp(decay_fwd_raw.astype(jnp.float32))
    lg_b = -jnp.exp(decay_bwd_raw.astype(jnp.float32))
    fwd = retention_chunkwise(qp, kp, vp, lg_f, True)
    flip = lambda t: jnp.flip(t, axis=1)
    bwd = flip(retention_chunkwise(flip(qp), flip(kp), flip(vp), lg_b, False))
    o = (fwd + bwd)[:, pad:]
    o = rms_norm(o)
    o = o.reshape(B, L, RET_V).astype(h.dtype) * jax.nn.silu(g)
    return o @ w_out


def diff_attn_mixer(h, w_in, w_out, lq1, lk1, lq2, lk2, subln, lambda_init):
    B, L, _ = h.shape
    proj = h @ w_in
    q, k, v, g = jnp.split(proj, [DIFF_QK, 2 * DIFF_QK, 2 * DIFF_QK + DIFF_V], axis=-1)
    q = q.reshape(B, L, DIFF_HEADS, 2, DIFF_DH)
    k = k.reshape(B, L, DIFF_HEADS, 2, DIFF_DH)
    v = v.reshape(B, L, DIFF_HEADS, DIFF_DV)
    pos = jnp.arange(L, dtype=jnp.float32)
    q = rotary(q, pos, DIFF_ROT, ROPE_THETA)
    k = rotary(k, pos, DIFF_ROT, ROPE_THETA)
    lam = (jnp.exp(jnp.sum(lq1.astype(jnp.float32) * lk1.astype(jnp.float32)))
           - jnp.exp(jnp.sum(lq2.astype(jnp.float32) * lk2.astype(jnp.float32))) + lambda_init)
    nb = -(-L // Q_BLOCK)
    Lq = nb * Q_BLOCK
    qb = jnp.pad(q, ((0, 0), (0, Lq - L), (0, 0), (0, 0), (0, 0))).reshape(B, nb, Q_BLOCK, DIFF_HEADS, 2, DIFF_DH)
    qb = jnp.moveaxis(qb, 1, 0)
    scale = DIFF_DH ** -0.5

    def block(qi):
        s = jnp.einsum('bqhmd,bkhmd->bhmqk', qi, k).astype(jnp.float32) * scale
        p = jax.nn.softmax(s, axis=-1)
        a = p[:, :, 0] - lam * p[:, :, 1]
        return jnp.einsum('bhqk,bkhe->bqhe', a.astype(v.dtype), v)

    o = lax.map(block, qb)
    o = jnp.moveaxis(o, 0, 1).reshape(B, Lq, DIFF_HEADS, DIFF_DV)[:, :L]
    o = rms_norm(o, subln, eps=1e-5) * (1.0 - lambda_init)
    o = o.reshape(B, L, DIFF_V) * jax.nn.silu(g)
    return o @ w_out


def setup_inputs(seed: int = 0) -> dict:
    key = jax.random.key(seed)
    ks = jax.random.split(key, 16)
    f32 = jnp.float32
    ret_in = 2 * RET_QK + 2 * RET_V
    diff_in = 2 * DIFF_QK + 2 * DIFF_V
    base = (-5.0 - jnp.arange(RET_HEADS, dtype=f32)) * math.log(2.0)
    return {
        'x_prompt': jax.random.normal(ks[0], (BATCH, SEQ, D_MODEL), f32),
        'x_sample': jax.random.normal(ks[1], (DEC_BATCH, DEC_SEQ, D_MODEL), f32),
        'meta_tokens': jax.random.normal(ks[2], (N_META, D_MODEL), f32),
        'pre_norm': 1.0 + 0.02 * jax.random.normal(ks[3], (DEPTH, D_MODEL), f32),
        'post_norm': 1.0 + 0.02 * jax.random.normal(ks[4], (DEPTH, D_MODEL), f32),
        'ret_w_in': jax.random.normal(ks[5], (N_RET, D_MODEL, ret_in), f32) * D_MODEL ** -0.5,
        'ret_w_out': jax.random.normal(ks[6], (N_RET, RET_V, D_MODEL), f32) * RET_V ** -0.5,
        'ret_decay_fwd': base[None, :] + 0.05 * jax.random.normal(ks[7], (N_RET, RET_HEADS), f32),
        'ret_decay_bwd': base[None, :] + 0.05 * jax.random.normal(ks[8], (N_RET, RET_HEADS), f32),
        'diff_w_in': jax.random.normal(ks[9], (N_DIFF, D_MODEL, diff_in), f32) * D_MODEL ** -0.5,
        'diff_w_out': jax.random.normal(ks[10], (N_DIFF, DIFF_V, D_MODEL), f32) * DIFF_V ** -0.5,
        'diff_lambda_q1': 0.1 * jax.random.normal(ks[11], (N_DIFF, DIFF_DH), f32),
        'diff_lambda_k1': 0.1 * jax.random.normal(ks[12], (N_DIFF, DIFF_DH), f32),
        'diff_lambda_q2': 0.1 * jax.random.normal(ks[13], (N_DIFF, DIFF_DH), f32),
        'diff_lambda_k2': 0.1 * jax.random.normal(ks[14], (N_DIFF, DIFF_DH), f32),
        'diff_subln': 1.0 + 0.02 * jax.random.normal(ks[15], (N_DIFF, DIFF_DV), f32),
    }


def reference(x_prompt, x_sample, meta_tokens, pre_norm, post_norm, ret_w_in, ret_w_out,
              ret_decay_fwd, ret_decay_bwd, diff_w_in, diff_w_out, diff_lambda_q1,
              diff_lambda_k1, diff_lambda_q2, diff_lambda_k2, diff_subln):
    def trunk(x):
        B = x.shape[0]
        meta = jnp.broadcast_to(meta_tokens.astype(x.dtype)[None], (B, N_META, D_MODEL))
        x = jnp.concatenate([meta, x], axis=1)
        for i in range(DEPTH):
            h = rms_norm(x, pre_norm[i])
            j = i // N_MIXERS
            if i % N_MIXERS == 0:
                m = retention_mixer(h, ret_w_in[j], ret_w_out[j], ret_decay_fwd[j], ret_decay_bwd[j])
            else:
                lambda_init = 0.8 - 0.6 * math.exp(-0.3 * i)
                m = diff_attn_mixer(h, diff_w_in[j], diff_w_out[j], diff_lambda_q1[j], diff_lambda_k1[j],
                                    diff_lambda_q2[j], diff_lambda_k2[j], diff_subln[j], lambda_init)
            x = x + rms_norm(m, post_norm[i])
        return x[:, N_META:]

    y_prompt = trunk(x_prompt)
    y_sample = trunk(x_sample)
    return (y_prompt, y_sample)
```

```python
import math
from contextlib import ExitStack
import numpy as np
import concourse.bass as bass
import concourse.mybir as mybir
from concourse.bass_utils import run_bass_kernel_spmd

F32 = mybir.dt.float32
BF16 = mybir.dt.bfloat16
AF = mybir.ActivationFunctionType
ALU = mybir.AluOpType
AX = mybir.AxisListType

FULL = dict(D=4096, HR=16, HD=32, S_P=2048, S_S=8192)
N_META = 16
PADR = 112
RET_THETA = 10000.0
ROPE_THETA = 500000.0
EPOCH = 28000


class Buf:
    __slots__ = ("w", "r")

    def __init__(self):
        self.w = {}
        self.r = {}


class _Eng:
    def __init__(self, name):
        self.name = name
        self.ops = []
        self.sem = None
        self.epoch = 0
        self.count = 0
        self.pending = False
        self.known = {}
        self.slots = []
        self.slot_i = 0
        self.last = None


class Sched:
    def __init__(self, nc, stack, n_slots):
        self.nc = nc
        self.stack = stack
        self.E = {n: _Eng(n) for n in ("pe", "act", "dve", "pool", "sp")}
        self.nsem = 0
        for n, e in self.E.items():
            e.sem = self._newsem(n)
        for n, k in n_slots.items():
            for i in range(k):
                self.E[n].slots.append([self._newsem(f"{n}d{i}"), 0, (n, "slot", i)])

    def _newsem(self, name):
        self.nsem += 1
        return self.stack.enter_context(self.nc.semaphore(f"s{self.nsem}_{name}"))

    def _need(self, eng, waits, key, sem, val):
        if val <= 0:
            return
        if eng.name == "pe" and key[0] == "pe" and key[1] != "slot":
            return
        if eng.known.get(key, 0) >= val:
            return
        cur = waits.get(key)
        if cur is None or cur[1] < val:
            waits[key] = (sem, val)

    def emit(self, en, fn, reads=(), writes=(), dma=False, signal=True):
        eng = self.E[en]
        waits = {}
        for b in reads:
            for k, (s, v) in b.w.items():
                self._need(eng, waits, k, s, v)
        for b in writes:
            for k, (s, v) in b.w.items():
                self._need(eng, waits, k, s, v)
            for k, (s, v) in b.r.items():
                self._need(eng, waits, k, s, v)
        if dma:
            slot = eng.slots[eng.slot_i]
            eng.slot_i = (eng.slot_i + 1) % len(eng.slots)
            self._need(eng, waits, slot[2], slot[0], slot[1])
            slot[1] += 16
            tok = (slot[2], slot[0], slot[1])
            inc = (slot[0], 16)
        else:
            if eng.count >= EPOCH and not eng.pending:
                eng.epoch += 1
                eng.count = 0
                eng.sem = self._newsem(f"{en}e{eng.epoch}")
            key = (en, eng.epoch)
            if signal:
                eng.count += 1
                eng.pending = False
                tok = (key, eng.sem, eng.count)
                inc = (eng.sem, 1)
            else:
                eng.pending = True
                tok = (key, eng.sem, eng.count + 1)
                inc = None
            eng.last = tok
        for k, (s, v) in waits.items():
            eng.known[k] = v
        eng.ops.append((list(waits.values()), fn, inc))
        k, s, v = tok
        for b in reads:
            cur = b.r.get(k)
            if cur is None or cur[1] < v:
                b.r[k] = (s, v)
        for b in writes:
            b.w = {k: (s, v)}
            b.r = {}
        return tok

    def barrier(self):
        toks = []
        for e in self.E.values():
            assert not e.pending
            if e.last is not None:
                toks.append(e.last)
            for slot in e.slots:
                if slot[1] > 0:
                    toks.append((slot[2], slot[0], slot[1]))
        for e in self.E.values():
            waits = {}
            for k, s, v in toks:
                if k[0] == e.name and k[1] != "slot":
                    continue
                if e.known.get(k, 0) >= v:
                    continue
                waits[k] = (s, v)
                e.known[k] = v
            if waits:
                e.ops.append((list(waits.values()), None, None))

    def flush(self, block):
        def mk(en):
            def run(engobj):
                for waits, fn, inc in self.E[en].ops:
                    for s, v in waits:
                        engobj.wait_ge(s, v)
                    if fn is None:
                        continue
                    ins = fn(engobj)
                    if inc is not None:
                        ins.then_inc(inc[0], inc[1])
                self.E[en].ops = []
            return run
        block.tensor(mk("pe"))
        block.scalar(mk("act"))
        block.vector(mk("dve"))
        block.gpsimd(mk("pool"))
        block.sync(mk("sp"))


def build_program(cfg, debug=False):
    D = cfg["D"]
    HR = cfg["HR"]
    HD = cfg["HD"]
    KC = D // 128
    RQK = HR * 256
    RV = HR * 512
    DQK = HD * 256
    DV = HD * 256
    assert RV == 2 * D and DV == 2 * D
    NR_IN = 2 * RQK + 2 * RV
    ND_IN = 2 * DQK + 2 * DV
    lam_init = 0.8 - 0.6 * math.exp(-0.3 * 1)
    seqs = [("s", cfg["S_S"]), ("p", cfg["S_P"])]
    TB_IN = cfg.get("TB_IN", 8)
    TB_OUT = cfg.get("TB_OUT", 4)

    nc = bass.Bass("TRN2", target_bir_lowering=False)
    okind = "ExternalOutput" if debug else "Internal"

    def dram(name, shape, dt, kind):
        return nc.dram_tensor(name, list(shape), dt, kind=kind).ap()

    xin, yout = {}, {}
    for nm, S in seqs:
        xin[nm] = dram(f"x_{nm}", [S, D], F32, "ExternalInput")
        yout[nm] = dram(f"y_{nm}", [S, D], F32, "ExternalOutput")
    meta = dram("meta", [N_META, D], F32, "ExternalInput")
    pre_n = dram("pre_norm", [2, D], F32, "ExternalInput")
    post_n = dram("post_norm", [2, D], F32, "ExternalInput")
    rwin = dram("ret_w_in", [D, NR_IN], F32, "ExternalInput")
    rwout = dram("ret_w_out", [RV, D], F32, "ExternalInput")
    rdec = dram("ret_decay", [2, HR], F32, "ExternalInput")
    dwin = dram("diff_w_in", [D, ND_IN], F32, "ExternalInput")
    dwout = dram("diff_w_out", [DV, D], F32, "ExternalInput")
    dlam = dram("diff_lam", [4, 128], F32, "ExternalInput")
    dsub = dram("diff_subln", [1, 256], F32, "ExternalInput")
    Lmax = max(S for _, S in seqs) + 128
    rtab = dram("rtab", [Lmax, 2, 128], F32, "ExternalInput")
    dtab = dram("dtab", [Lmax, 2, 16], F32, "ExternalInput")
    ctab = dram("ctab", [128, 6, 128], F32, "ExternalInput")
    ptab = dram("ptab", [128, 4], F32, "ExternalInput")
    ident_in = dram("ident", [128, 128], F32, "ExternalInput")

    scr = {}
    dbufs = {}
    for nm, S in seqs:
        Lp = S + 128
        scr[nm] = dict(
            q=dram(f"q_{nm}", [Lp, 2 * D], BF16, okind),
            k=dram(f"k_{nm}", [Lp, 2 * D], BF16, okind),
            v=dram(f"v_{nm}", [Lp, 2 * D], BF16, okind),
            g=dram(f"g_{nm}", [Lp, 2 * D], BF16, okind),
            o=dram(f"o_{nm}", [Lp, 2 * D], BF16, okind),
            m=dram(f"m_{nm}", [Lp, D], F32, okind),
            x1=dram(f"x1_{nm}", [Lp, D], F32, okind),
        )

    def dbuf(*key):
        b = dbufs.get(key)
        if b is None:
            b = dbufs[key] = Buf()
        return b

    top = ExitStack()
    with top:
        sch = Sched(nc, top, {"sp": 16, "pool": 12})
        emit = sch.emit

        uid = [0]

        def tsb(stack, name, shape, dt):
            uid[0] += 1
            return stack.enter_context(nc.sbuf_tensor(f"sb{uid[0]}_{name}", list(shape), dt))

        ident = tsb(top, "ident", [128, 128], BF16)
        ctab_s = tsb(top, "ctab_s", [128, 6, 128], F32)
        ptab_s = tsb(top, "ptab_s", [128, 4], F32)
        subln_s = tsb(top, "subln_s", [128, 256], F32)
        lg = tsb(top, "lg", [128, 2, HR], F32)
        cdec = tsb(top, "cdec", [128, 2, HR], F32)
        lam_t = tsb(top, "lam_t", [128, 8], F32)
        zero_c = tsb(top, "zero_c", [128, 1], F32)
        gain_t = tsb(top, "gain_t", [128, D], F32)
        gain_b = Buf()
        CB = Buf()
        CONST = [CB]

        PSB = [top.enter_context(nc.psum_tensor(f"psb{i}", [128, 512], F32)) for i in range(6)]
        PST = [top.enter_context(nc.psum_tensor(f"pst{i}", [128, 8, 128], BF16)) for i in range(2)]
        PSB_b = [Buf() for _ in range(6)]
        PST_b = [Buf() for _ in range(2)]
        bank_i = [0]
        tp_i = [0]

        def nextbank():
            i = bank_i[0]
            bank_i[0] = (i + 1) % 6
            return i

        class TPool:
            def __init__(self, stack, name, shape, dt, n):
                self.t = [tsb(stack, f"{name}{i}", shape, dt) for i in range(n)]
                self.b = [Buf() for _ in range(n)]
                self.i = 0

            def get(self):
                i = self.i
                self.i = (i + 1) % len(self.t)
                return self.t[i], self.b[i]

        def run_phase(fn):
            with ExitStack() as st, nc.Block() as block:
                fn(st)
                sch.barrier()
                sch.flush(block)

        def ph_const(st):
            ident_f = tsb(st, "ident_f", [128, 128], F32)
            dec_raw = tsb(st, "dec_raw", [128, 2, HR], F32)
            lam_in = tsb(st, "lam_in", [128, 4, 128], F32)
            B0 = Buf()
            emit("sp", lambda e: e.dma_start(out=ident_f[:], in_=ident_in[:, :]), writes=[B0], dma=True)
            emit("sp", lambda e: e.dma_start(out=ctab_s[:], in_=ctab[:, :, :]), writes=[CB], dma=True)
            emit("sp", lambda e: e.dma_start(out=ptab_s[:], in_=ptab[:, :]), writes=[CB], dma=True)
            emit("sp", lambda e: e.dma_start(out=subln_s[:], in_=dsub[0:1, :].partition_broadcast(128)), writes=[CB], dma=True)
            for i in range(2):
                emit("sp", lambda e, i=i: e.dma_start(out=dec_raw[:, i, :], in_=rdec[i:i + 1, :].partition_broadcast(128)),
                     writes=[B0], dma=True)
            for i in range(4):
                emit("sp", lambda e, i=i: e.dma_start(out=lam_in[:, i, :], in_=dlam[i:i + 1, :].partition_broadcast(128)),
                     writes=[B0], dma=True)
            emit("dve", lambda e: e.tensor_copy(out=ident[:], in_=ident_f[:]), reads=[B0], writes=[CB])
            emit("dve", lambda e: e.memset(zero_c[:], 0.0), writes=[CB])
            emit("act", lambda e: e.activation(out=lg[:], in_=dec_raw[:], func=AF.Exp), reads=[B0], writes=[CB])
            emit("dve", lambda e: e.tensor_scalar(out=lg[:], in0=lg[:], scalar1=-1.0, scalar2=None, op0=ALU.mult),
                 reads=[CB], writes=[CB])
            emit("act", lambda e: e.activation(out=cdec[:], in_=lg[:], func=AF.Exp, scale=128.0), reads=[CB], writes=[CB])
            emit("dve", lambda e: e.tensor_tensor(out=lam_in[:, 0, :], in0=lam_in[:, 0, :], in1=lam_in[:, 1, :], op=ALU.mult),
                 reads=[B0], writes=[B0])
            emit("dve", lambda e: e.tensor_tensor(out=lam_in[:, 2, :], in0=lam_in[:, 2, :], in1=lam_in[:, 3, :], op=ALU.mult),
                 reads=[B0], writes=[B0])
            emit("dve", lambda e: e.reduce_sum(out=lam_t[:, 0:1], in_=lam_in[:, 0, :], axis=AX.X), reads=[B0], writes=[CB])
            emit("dve", lambda e: e.reduce_sum(out=lam_t[:, 1:2], in_=lam_in[:, 2, :], axis=AX.X), reads=[B0], writes=[CB])
            emit("act", lambda e: e.activation(out=lam_t[:, 2:4], in_=lam_t[:, 0:2], func=AF.Exp), reads=[CB], writes=[CB])
            emit("dve", lambda e: e.tensor_tensor(out=lam_t[:, 4:5], in0=lam_t[:, 3:4], in1=lam_t[:, 2:3], op=ALU.subtract),
                 reads=[CB], writes=[CB])
            emit("dve", lambda e: e.tensor_scalar(out=lam_t[:, 4:5], in0=lam_t[:, 4:5], scalar1=-lam_init, scalar2=None,
                                                  op0=ALU.add), reads=[CB], writes=[CB])

        def load_gain(row_ap):
            emit("sp", lambda e: e.dma_start(out=gain_t[:], in_=row_ap.partition_broadcast(128)), writes=[gain_b], dma=True)

        def rms_rstd(stat, junk_ap, junk_b, src_ap, src_b, ncols, eps):
            stt, st_b = stat.get()
            emit("act", lambda e: e.activation(out=junk_ap, in_=src_ap, func=AF.Square, accum_out=stt[:, 0:1]),
                 reads=[src_b], writes=[junk_b, st_b])
            emit("dve", lambda e: e.tensor_scalar(out=stt[:, 1:2], in0=stt[:, 0:1], scalar1=1.0 / ncols, scalar2=eps,
                                                  op0=ALU.mult, op1=ALU.add), reads=[st_b], writes=[st_b])
            emit("act", lambda e: e.activation(out=stt[:, 2:3], in_=stt[:, 1:2], func=AF.Sqrt), reads=[st_b], writes=[st_b])
            emit("dve", lambda e: e.reciprocal(out=stt[:, 3:4], in_=stt[:, 2:3]), reads=[st_b], writes=[st_b])
            return stt, st_b

        def transposes(src_blocks, src_b, dst_fn, dst_bufs, evac="dve"):
            i0 = 0
            nblk = len(src_blocks)
            while i0 < nblk:
                n = min(8, nblk - i0)
                bk = tp_i[0] % 2
                tp_i[0] += 1
                ptv = PST[bk]
                for j in range(n):
                    src = src_blocks[i0 + j]
                    emit("pe", lambda e, j=j, src=src, ptv=ptv: e.transpose(ptv[:, j, :], src, ident[:]),
                         reads=[*src_b, *CONST], writes=[PST_b[bk]], signal=(j == n - 1))
                dst = dst_fn(i0, n)
                if evac == "act":
                    emit(evac, lambda e, dst=dst, ptv=ptv, n=n: e.copy(out=dst, in_=ptv[:, 0:n, :]),
                         reads=[PST_b[bk]], writes=dst_bufs)
                else:
                    emit(evac, lambda e, dst=dst, ptv=ptv, n=n: e.tensor_copy(out=dst, in_=ptv[:, 0:n, :]),
                         reads=[PST_b[bk]], writes=dst_bufs)
                i0 += n

        def x_rows(nm, layer, t):
            if layer == 0:
                return xin[nm][(t - 1) * 128:t * 128, :] if t > 0 else None
            return scr[nm]["x1"][t * 128:(t + 1) * 128, :]

        def load_x(nm, layer, t, tl, tb):
            if layer == 0 and t == 0:
                emit("dve", lambda e: e.memset(tl[:], 0.0), writes=[tb])
                emit("sp", lambda e: e.dma_start(out=tl[PADR:128, :], in_=meta[:, :]), writes=[tb], dma=True)
            else:
                src = x_rows(nm, layer, t)
                rb = [dbuf(nm, "x1", t)] if layer == 1 else []
                emit("sp", lambda e: e.dma_start(out=tl[:], in_=src), reads=rb, writes=[tb], dma=True)

        def ph_inproj(nm, layer):
            S = dict(seqs)[nm]
            NT = S // 128 + 1
            win = rwin if layer == 0 else dwin
            QK = RQK if layer == 0 else DQK
            VV = RV if layer == 0 else DV
            NIN = 2 * QK + 2 * VV

            def body(st):
                AT = tsb(st, "AT", [128, KC, TB_IN * 128], BF16)
                AT_b = [Buf() for _ in range(TB_IN)]
                WT = TPool(st, "WT", [128, KC, 512], BF16, 2)
                xrow = TPool(st, "xrow", [128, D], F32, 1)
                hrow = TPool(st, "hrow", [128, D], BF16, 2)
                stat = TPool(st, "stat", [128, 8], F32, 4)
                stg = TPool(st, "stg", [128, 512], F32, 3)
                obf = TPool(st, "obf", [128, 512], BF16, 4)
                rot = TPool(st, "rot", [128, 4, 128], F32, 2)
                tw = 128 if layer == 0 else 16
                tabs = tsb(st, "tabs", [128, TB_IN, 2, tw], F32)
                tabs_b = Buf()
                tabd = rtab if layer == 0 else dtab
                load_gain(pre_n[layer:layer + 1, :])
                for blk in range(0, NT, TB_IN):
                    tiles = list(range(blk, min(NT, blk + TB_IN)))
                    nt = len(tiles)
                    emit("sp", lambda e, blk=blk, nt=nt: e.dma_start(
                        out=tabs[:, 0:nt, :, :],
                        in_=tabd[blk * 128:(blk + nt) * 128, :, :].rearrange("(t p) a c -> p t a c", p=128)),
                        writes=[tabs_b], dma=True)
                    for ti, t in enumerate(tiles):
                        xt, xb = xrow.get()
                        load_x(nm, layer, t, xt, xb)
                        ht, hb = hrow.get()
                        stt, st_b = rms_rstd(stat, ht[:], hb, xt[:], xb, D, 1e-6)
                        emit("dve", lambda e, ht=ht, xt=xt, stt=stt: e.scalar_tensor_tensor(
                            out=ht[:], in0=xt[:], scalar=stt[:, 3:4], in1=gain_t[:], op0=ALU.mult, op1=ALU.mult),
                            reads=[xb, st_b, gain_b], writes=[hb])
                        blocks = [ht[:, c * 128:(c + 1) * 128] for c in range(KC)]
                        transposes(blocks, [hb],
                                   lambda i0, n, ti=ti: AT[:, i0:i0 + n, ti * 128:(ti + 1) * 128], [AT_b[ti]])
                    for cb in range(NIN // 512):
                        c0 = cb * 512
                        if c0 < QK:
                            sec, sc0 = "q", c0
                        elif c0 < 2 * QK:
                            sec, sc0 = "k", c0 - QK
                        elif c0 < 2 * QK + VV:
                            sec, sc0 = "v", c0 - 2 * QK
                        else:
                            sec, sc0 = "g", c0 - 2 * QK - VV
                        wt, wb = WT.get()
                        nparts = max(1, KC // 8)
                        per = KC // nparts
                        toks = {}
                        for part in range(nparts):
                            src = win[part * per * 128:(part + 1) * per * 128, c0:c0 + 512].rearrange(
                                "(a p) c -> p a c", p=128)
                            tmpb = Buf()
                            if part == 0:
                                tmpb.w, tmpb.r = wb.w, wb.r
                            tok = emit("pool", lambda e, wt=wt, src=src, part=part, per=per: e.dma_start(
                                out=wt[:, part * per:(part + 1) * per, :], in_=src), writes=[tmpb], dma=True)
                            toks[tok[0]] = (tok[1], tok[2])
                        wb.w, wb.r = toks, {}
                        for ti, t in enumerate(tiles):
                            bk = nextbank()
                            for kc in range(KC):
                                emit("pe", lambda e, bk=bk, kc=kc, ti=ti, wt=wt: e.matmul(
                                    PSB[bk][:, :], AT[:, kc, ti * 128:(ti + 1) * 128], wt[:, kc, :],
                                    start=(kc == 0), stop=(kc == KC - 1)),
                                    reads=[AT_b[ti], wb], writes=[PSB_b[bk]], signal=(kc == KC - 1))
                            ob, ob_b = obf.get()
                            if sec == "v":
                                emit("dve", lambda e, ob=ob, bk=bk: e.tensor_copy(out=ob[:], in_=PSB[bk][:, :]),
                                     reads=[PSB_b[bk]], writes=[ob_b])
                            elif sec == "g":
                                emit("act", lambda e, ob=ob, bk=bk: e.activation(out=ob[:], in_=PSB[bk][:, :], func=AF.Silu),
                                     reads=[PSB_b[bk]], writes=[ob_b])
                            else:
                                sg, sg_b = stg.get()
                                scl = (1.0 / 16.0) if (layer == 0 and sec == "q") else 1.0
                                emit("act", lambda e, sg=sg, bk=bk, scl=scl: e.activation(
                                    out=sg[:], in_=PSB[bk][:, :], func=AF.Copy, scale=scl),
                                    reads=[PSB_b[bk]], writes=[sg_b])
                                if layer == 1:
                                    emit("dve", lambda e, ob=ob, sg=sg: e.tensor_copy(out=ob[:], in_=sg[:]),
                                         reads=[sg_b], writes=[ob_b])
                                ngrp = 2 if layer == 0 else 4
                                gw = 512 // ngrp
                                hw = 128 if layer == 0 else 16
                                for gi in range(ngrp):
                                    r, r_b = rot.get()
                                    x1 = sg[:, gi * gw:gi * gw + hw]
                                    x2 = sg[:, gi * gw + hw:gi * gw + 2 * hw]
                                    cs = tabs[:, ti, 0, :]
                                    sn = tabs[:, ti, 1, :]
                                    o1 = ob[:, gi * gw:gi * gw + hw]
                                    o2 = ob[:, gi * gw + hw:gi * gw + 2 * hw]
                                    T = [r[:, i, 0:hw] for i in range(4)]
                                    rd = [sg_b, tabs_b, r_b]
                                    emit("dve", lambda e, T=T, x1=x1, cs=cs: e.tensor_tensor(out=T[0], in0=x1, in1=cs, op=ALU.mult),
                                         reads=rd, writes=[r_b])
                                    emit("dve", lambda e, T=T, x2=x2, sn=sn: e.tensor_tensor(out=T[1], in0=x2, in1=sn, op=ALU.mult),
                                         reads=rd, writes=[r_b])
                                    emit("dve", lambda e, T=T, x2=x2, cs=cs: e.tensor_tensor(out=T[2], in0=x2, in1=cs, op=ALU.mult),
                                         reads=rd, writes=[r_b])
                                    emit("dve", lambda e, T=T, x1=x1, sn=sn: e.tensor_tensor(out=T[3], in0=x1, in1=sn, op=ALU.mult),
                                         reads=rd, writes=[r_b])
                                    emit("dve", lambda e, T=T, o1=o1: e.tensor_tensor(out=o1, in0=T[0], in1=T[1], op=ALU.subtract),
                                         reads=[r_b], writes=[ob_b])
                                    emit("dve", lambda e, T=T, o2=o2: e.tensor_tensor(out=o2, in0=T[2], in1=T[3], op=ALU.add),
                                         reads=[r_b], writes=[ob_b])
                            dst = scr[nm][sec][t * 128:(t + 1) * 128, sc0:sc0 + 512]
                            emit("pool", lambda e, dst=dst, ob=ob: e.dma_start(out=dst, in_=ob[:]),
                                 reads=[ob_b], writes=[dbuf(nm, sec, t)], dma=True)
            return body

        def ph_ret(nm):
            S = dict(seqs)[nm]
            NT = S // 128 + 1

            def body(st):
                oacc = tsb(st, "oacc", [128, NT, 512], F32)
                oacc_b = [Buf() for _ in range(NT)]
                Sst = [tsb(st, f"Sst{d}", [128, 2, 512], F32) for d in range(2)]
                Sbf = [tsb(st, f"Sbf{d}", [128, 2, 512], BF16) for d in range(2)]
                Sst_b = [Buf(), Buf()]
                Sbf_b = [Buf(), Buf()]
                htab = tsb(st, "htab", [128, 5, 128], F32)
                kd = tsb(st, "kd", [128, 2], F32)
                htab_b = Buf()
                qkv = TPool(st, "qkv", [128, 1024], BF16, 3)
                qT = TPool(st, "qT", [128, 4, 128], BF16, 3)
                qTd = TPool(st, "qTd", [128, 2, 128], BF16, 3)
                kdt = TPool(st, "kdt", [128, 256], BF16, 3)
                SD = TPool(st, "SD", [128, 128], BF16, 3)
                gl = TPool(st, "gl", [128, 512], BF16, 2)
                ob = TPool(st, "ob", [128, 512], BF16, 2)
                junk = TPool(st, "junk", [128, 512], F32, 2)
                stat = TPool(st, "stat", [128, 8], F32, 4)
                for h in range(HR):
                    lf = lg[:, 0, h:h + 1]
                    lb = lg[:, 1, h:h + 1]
                    emit("act", lambda e, lf=lf: e.activation(out=htab[:, 0, :], in_=ctab_s[:, 0, :], func=AF.Exp, scale=lf),
                         reads=CONST, writes=[htab_b])
                    emit("act", lambda e, lb=lb: e.activation(out=htab[:, 1, :], in_=ctab_s[:, 2, :], func=AF.Exp, scale=lb),
                         reads=CONST, writes=[htab_b])
                    emit("act", lambda e, lf=lf: e.activation(out=htab[:, 3, :], in_=ctab_s[:, 4, :], func=AF.Exp, scale=lf),
                         reads=CONST, writes=[htab_b])
                    emit("act", lambda e, lb=lb: e.activation(out=htab[:, 4, :], in_=ctab_s[:, 5, :], func=AF.Exp, scale=lb),
                         reads=CONST, writes=[htab_b])
                    emit("act", lambda e, lf=lf: e.activation(out=kd[:, 0:1], in_=ptab_s[:, 0:1], func=AF.Exp, scale=lf),
                         reads=CONST, writes=[htab_b])
                    emit("act", lambda e, lb=lb: e.activation(out=kd[:, 1:2], in_=ptab_s[:, 1:2], func=AF.Exp, scale=lb),
                         reads=CONST, writes=[htab_b])
                    emit("dve", lambda e: e.tensor_tensor(out=htab[:, 0, :], in0=htab[:, 0, :], in1=ctab_s[:, 1, :], op=ALU.mult),
                         reads=[htab_b, *CONST], writes=[htab_b])
                    emit("dve", lambda e: e.tensor_tensor(out=htab[:, 1, :], in0=htab[:, 1, :], in1=ctab_s[:, 3, :], op=ALU.mult),
                         reads=[htab_b, *CONST], writes=[htab_b])
                    emit("dve", lambda e: e.tensor_tensor(out=htab[:, 2, :], in0=htab[:, 0, :], in1=htab[:, 1, :], op=ALU.add),
                         reads=[htab_b], writes=[htab_b])
                    for d in range(2):
                        emit("dve", lambda e, d=d: e.memset(Sst[d][:], 0.0), writes=[Sst_b[d]])
                        emit("dve", lambda e, d=d: e.memset(Sbf[d][:], 0.0), writes=[Sbf_b[d]])
                    for d in range(2):
                        order = range(NT) if d == 0 else range(NT - 1, -1, -1)
                        for n in order:
                            tl, tl_b = qkv.get()
                            r0 = n * 128
                            emit("sp", lambda e, tl=tl, r0=r0, h=h: e.dma_start(
                                out=tl[:, 0:256], in_=scr[nm]["q"][r0:r0 + 128, h * 256:(h + 1) * 256]),
                                reads=[dbuf(nm, "q", n)], writes=[tl_b], dma=True)
                            emit("sp", lambda e, tl=tl, r0=r0, h=h: e.dma_start(
                                out=tl[:, 256:512], in_=scr[nm]["k"][r0:r0 + 128, h * 256:(h + 1) * 256]),
                                reads=[dbuf(nm, "k", n)], writes=[tl_b], dma=True)
                            emit("sp", lambda e, tl=tl, r0=r0, h=h: e.dma_start(
                                out=tl[:, 512:1024], in_=scr[nm]["v"][r0:r0 + 128, h * 512:(h + 1) * 512]),
                                reads=[dbuf(nm, "v", n)], writes=[tl_b], dma=True)
                            qt, qt_b = qT.get()
                            nb = 4 if d == 0 else 2
                            transposes([tl[:, c * 128:(c + 1) * 128] for c in range(nb)], [tl_b],
                                       lambda i0, n_, qt=qt: qt[:, i0:i0 + n_, :], [qt_b], evac="act")
                            qd, qd_b = qTd.get()
                            Rt = htab[:, 3 + d, :]
                            for c in range(2):
                                emit("dve", lambda e, qd=qd, qt=qt, c=c, Rt=Rt: e.tensor_tensor(
                                    out=qd[:, c, :], in0=qt[:, c, :], in1=Rt, op=ALU.mult),
                                    reads=[qt_b, htab_b], writes=[qd_b])
                            kt_, kt_b = kdt.get()
                            emit("dve", lambda e, kt_=kt_, tl=tl, d=d: e.tensor_scalar(
                                out=kt_[:], in0=tl[:, 256:512], scalar1=kd[:, d:d + 1], scalar2=None, op0=ALU.mult),
                                reads=[tl_b, htab_b], writes=[kt_b])
                            bo = nextbank()
                            if d == 0:
                                bs = nextbank()
                                for c in range(2):
                                    emit("pe", lambda e, bs=bs, qt=qt, c=c: e.matmul(
                                        PSB[bs][:, 0:128], qt[:, 2 + c, :], qt[:, c, :], start=(c == 0), stop=(c == 1)),
                                        reads=[qt_b], writes=[PSB_b[bs]], signal=(c == 1))
                                sd, sd_b = SD.get()
                                emit("dve", lambda e, sd=sd, bs=bs: e.tensor_tensor(
                                    out=sd[:], in0=PSB[bs][:, 0:128], in1=htab[:, 2, :], op=ALU.mult),
                                    reads=[PSB_b[bs], htab_b], writes=[sd_b])
                                emit("pe", lambda e, bo=bo, sd=sd, tl=tl: e.matmul(
                                    PSB[bo][:, :], sd[:], tl[:, 512:1024], start=True, stop=False),
                                    reads=[sd_b, tl_b], writes=[PSB_b[bo]], signal=False)
                            for c in range(2):
                                emit("pe", lambda e, bo=bo, qd=qd, c=c, d=d: e.matmul(
                                    PSB[bo][:, :], qd[:, c, :], Sbf[d][:, c, :], start=(d == 1 and c == 0), stop=(c == 1)),
                                    reads=[qd_b, Sbf_b[d]], writes=[PSB_b[bo]], signal=(c == 1))
                            if d == 0:
                                emit("act", lambda e, n=n, bo=bo: e.activation(out=oacc[:, n, :], in_=PSB[bo][:, :], func=AF.Copy),
                                     reads=[PSB_b[bo]], writes=[oacc_b[n]])
                            else:
                                emit("dve", lambda e, n=n, bo=bo: e.tensor_tensor(
                                    out=oacc[:, n, :], in0=oacc[:, n, :], in1=PSB[bo][:, :], op=ALU.add),
                                    reads=[PSB_b[bo], oacc_b[n]], writes=[oacc_b[n]])
                            for c in range(2):
                                bu = nextbank()
                                emit("pe", lambda e, bu=bu, kt_=kt_, tl=tl, c=c: e.matmul(
                                    PSB[bu][:, :], kt_[:, c * 128:(c + 1) * 128], tl[:, 512:1024], start=True, stop=True),
                                    reads=[kt_b, tl_b], writes=[PSB_b[bu]])
                                emit("dve", lambda e, bu=bu, c=c, d=d, h=h: e.scalar_tensor_tensor(
                                    out=Sst[d][:, c, :], in0=Sst[d][:, c, :], scalar=cdec[:, d, h:h + 1], in1=PSB[bu][:, :],
                                    op0=ALU.mult, op1=ALU.add),
                                    reads=[PSB_b[bu], Sst_b[d], *CONST], writes=[Sst_b[d]])
                            emit("act", lambda e, d=d: e.activation(out=Sbf[d][:], in_=Sst[d][:], func=AF.Copy),
                                 reads=[Sst_b[d]], writes=[Sbf_b[d]])
                    for n in range(NT):
                        g_, g_b = gl.get()
                        r0 = n * 128
                        emit("sp", lambda e, g_=g_, r0=r0, h=h: e.dma_start(
                            out=g_[:], in_=scr[nm]["g"][r0:r0 + 128, h * 512:(h + 1) * 512]),
                            reads=[dbuf(nm, "g", n)], writes=[g_b], dma=True)
                        jk, jk_b = junk.get()
                        stt, st_b = rms_rstd(stat, jk[:], jk_b, oacc[:, n, :], oacc_b[n], 512, 1e-6)
                        o_, o_b = ob.get()
                        emit("dve", lambda e, o_=o_, n=n, stt=stt, g_=g_: e.scalar_tensor_tensor(
                            out=o_[:], in0=oacc[:, n, :], scalar=stt[:, 3:4], in1=g_[:], op0=ALU.mult, op1=ALU.mult),
                            reads=[oacc_b[n], st_b, g_b], writes=[o_b])
                        dst = scr[nm]["o"][r0:r0 + 128, h * 512:(h + 1) * 512]
                        emit("pool", lambda e, dst=dst, o_=o_: e.dma_start(out=dst, in_=o_[:]),
                             reads=[o_b], writes=[dbuf(nm, "o", n)], dma=True)
            return body

        def ph_att(nm):
            S = dict(seqs)[nm]
            NT = S // 128 + 1
            Lp = NT * 128
            scale = 128.0 ** -0.5

            def body(st):
                kload = tsb(st, "kload", [128, NT, 256], BF16)
                NCH = (NT + 7) // 8
                kload_b = [Buf() for _ in range(NCH)]
                KT = tsb(st, "KT", [128, 2, Lp], BF16)
                KT_b = Buf()
                Vx = tsb(st, "Vx", [128, NT, 257], BF16)
                Vx_b = [Buf() for _ in range(NCH)]
                qload = TPool(st, "qload", [128, 4, 256], BF16, 2)
                QT = TPool(st, "QT", [128, 2, 512], BF16, 2)
                gl = TPool(st, "gl", [128, 4, 256], BF16, 2)
                Pt = TPool(st, "Pt", [128, 512], BF16, 4)
                o0 = TPool(st, "o0", [128, 4, 256], F32, 2)
                tmpo = TPool(st, "tmpo", [128, 256], F32, 2)
                ob = TPool(st, "ob", [128, 256], BF16, 3)
                junk = TPool(st, "junk", [128, 256], F32, 2)
                stat = TPool(st, "stat", [128, 8], F32, 6)
                emit("dve", lambda e: e.memset(Vx[:, :, 256:257], 1.0), writes=Vx_b)
                sbank = [0]
                for h in range(HD):
                    hc = slice(h * 256, (h + 1) * 256)
                    rdk = [dbuf(nm, "k", t) for t in range(NT)]
                    rdv = [dbuf(nm, "v", t) for t in range(NT)]
                    for ch in range(NCH):
                        ta, tb_ = ch * 8, min(NT, ch * 8 + 8)
                        emit("sp", lambda e, hc=hc, ta=ta, tb_=tb_: e.dma_start(
                            out=kload[:, ta:tb_, :], in_=scr[nm]["k"][ta * 128:tb_ * 128, hc].rearrange("(t p) c -> p t c", p=128)),
                            reads=rdk[ta:tb_], writes=[kload_b[ch]], dma=True)
                        emit("sp", lambda e, hc=hc, ta=ta, tb_=tb_: e.dma_start(
                            out=Vx[:, ta:tb_, 0:256], in_=scr[nm]["v"][ta * 128:tb_ * 128, hc].rearrange("(t p) c -> p t c", p=128)),
                            reads=rdv[ta:tb_], writes=[Vx_b[ch]], dma=True)
                    for t0 in range(0, NT, 4):
                        n4 = min(4, NT - t0)
                        blocks = [kload[:, t0 + tt, m * 128:(m + 1) * 128] for tt in range(n4) for m in range(2)]

                        def dstf(i0, n_, t0=t0, n4=n4):
                            return KT[:, :, t0 * 128:(t0 + n4) * 128].rearrange("p m (t c) -> p t m c", c=128)
                        bk = tp_i[0] % 2
                        tp_i[0] += 1
                        ptv = PST[bk]
                        for j, src in enumerate(blocks):
                            emit("pe", lambda e, j=j, src=src, ptv=ptv: e.transpose(ptv[:, j, :], src, ident[:]),
                                 reads=[kload_b[t0 // 8], *CONST], writes=[PST_b[bk]], signal=(j == len(blocks) - 1))
                        for m in range(2):
                            emit("dve", lambda e, ptv=ptv, t0=t0, n4=n4, m=m: e.tensor_copy(
                                out=KT[:, m, t0 * 128:(t0 + n4) * 128].rearrange("p (t c) -> p t c", c=128),
                                in_=ptv[:, 0:2 * n4, :].rearrange("p (t m) c -> p t m c", m=2)[:, :, m, :]),
                                reads=[PST_b[bk]], writes=[KT_b])
                    for q0 in range(0, NT, 4):
                        nq = min(4, NT - q0)
                        ql, ql_b = qload.get()
                        g_, g_b = gl.get()
                        emit("sp", lambda e, ql=ql, q0=q0, nq=nq, hc=hc: e.dma_start(
                            out=ql[:, 0:nq, :], in_=scr[nm]["q"][q0 * 128:(q0 + nq) * 128, hc].rearrange("(t p) c -> p t c", p=128)),
                            reads=[dbuf(nm, "q", q0 + i) for i in range(nq)], writes=[ql_b], dma=True)
                        emit("sp", lambda e, g_=g_, q0=q0, nq=nq, hc=hc: e.dma_start(
                            out=g_[:, 0:nq, :], in_=scr[nm]["g"][q0 * 128:(q0 + nq) * 128, hc].rearrange("(t p) c -> p t c", p=128)),
                            reads=[dbuf(nm, "g", q0 + i) for i in range(nq)], writes=[g_b], dma=True)
                        qt, qt_b = QT.get()
                        bk = tp_i[0] % 2
                        tp_i[0] += 1
                        ptv = PST[bk]
                        blocks = [ql[:, tt, m * 128:(m + 1) * 128] for tt in range(nq) for m in range(2)]
                        for j, src in enumerate(blocks):
                            emit("pe", lambda e, j=j, src=src, ptv=ptv: e.transpose(ptv[:, j, :], src, ident[:]),
                                 reads=[ql_b, *CONST], writes=[PST_b[bk]], signal=(j == len(blocks) - 1))
                        for m in range(2):
                            emit("act", lambda e, ptv=ptv, nq=nq, m=m, qt=qt: e.activation(
                                out=qt[:, m, 0:nq * 128].rearrange("p (t c) -> p t c", c=128),
                                in_=ptv[:, 0:2 * nq, :].rearrange("p (t m) c -> p t m c", m=2)[:, :, m, :], func=AF.Copy),
                                reads=[PST_b[bk]], writes=[qt_b])
                        ot, ot_b = o0.get()
                        for m in range(2):
                            for kt in range(NT):
                                bs = 4 + (sbank[0] % 2)
                                sbank[0] += 1
                                emit("pe", lambda e, bs=bs, m=m, kt=kt, qt=qt, nq=nq: e.matmul(
                                    PSB[bs][:, 0:nq * 128], KT[:, m, kt * 128:(kt + 1) * 128], qt[:, m, 0:nq * 128],
                                    start=True, stop=True), reads=[KT_b, qt_b], writes=[PSB_b[bs]])
                                pt, pt_b = Pt.get()
                                bias = ptab_s[:, 2:3] if kt == 0 else zero_c[:, 0:1]
                                emit("act", lambda e, pt=pt, bs=bs, nq=nq, bias=bias: e.activation(
                                    out=pt[:, 0:nq * 128], in_=PSB[bs][:, 0:nq * 128], func=AF.Exp, bias=bias, scale=scale),
                                    reads=[PSB_b[bs], *CONST], writes=[pt_b])
                                for qi in range(nq):
                                    emit("pe", lambda e, qi=qi, pt=pt, kt=kt: e.matmul(
                                        PSB[qi][:, 0:257], pt[:, qi * 128:(qi + 1) * 128], Vx[:, kt, :],
                                        start=(kt == 0), stop=(kt == NT - 1)),
                                        reads=[pt_b, Vx_b[kt // 8]], writes=[PSB_b[qi]], signal=(kt == NT - 1))
                            for qi in range(nq):
                                stt, st_b = stat.get()
                                emit("dve", lambda e, stt=stt, qi=qi: e.reciprocal(out=stt[:, 0:1], in_=PSB[qi][:, 256:257]),
                                     reads=[PSB_b[qi]], writes=[st_b])
                                if m == 0:
                                    emit("dve", lambda e, ot=ot, qi=qi, stt=stt: e.tensor_scalar(
                                        out=ot[:, qi, :], in0=PSB[qi][:, 0:256], scalar1=stt[:, 0:1], scalar2=None, op0=ALU.mult),
                                        reads=[PSB_b[qi], st_b], writes=[ot_b])
                                else:
                                    emit("dve", lambda e, stt=stt: e.tensor_tensor(
                                        out=stt[:, 1:2], in0=stt[:, 0:1], in1=lam_t[:, 4:5], op=ALU.mult),
                                        reads=[st_b, *CONST], writes=[st_b])
                                    emit("dve", lambda e, ot=ot, qi=qi, stt=stt: e.scalar_tensor_tensor(
                                        out=ot[:, qi, :], in0=PSB[qi][:, 0:256], scalar=stt[:, 1:2], in1=ot[:, qi, :],
                                        op0=ALU.mult, op1=ALU.add),
                                        reads=[PSB_b[qi], st_b, ot_b], writes=[ot_b])
                        for qi in range(nq):
                            jk, jk_b = junk.get()
                            stt, st_b = rms_rstd(stat, jk[:], jk_b, ot[:, qi, :], ot_b, 256, 1e-5)
                            tm, tm_b = tmpo.get()
                            emit("dve", lambda e, tm=tm, ot=ot, qi=qi, stt=stt: e.scalar_tensor_tensor(
                                out=tm[:], in0=ot[:, qi, :], scalar=stt[:, 3:4], in1=subln_s[:], op0=ALU.mult, op1=ALU.mult),
                                reads=[ot_b, st_b, *CONST], writes=[tm_b])
                            o_, o_b = ob.get()
                            emit("dve", lambda e, o_=o_, tm=tm, g_=g_, qi=qi: e.scalar_tensor_tensor(
                                out=o_[:], in0=tm[:], scalar=float(1.0 - lam_init), in1=g_[:, qi, :], op0=ALU.mult, op1=ALU.mult),
                                reads=[tm_b, g_b], writes=[o_b])
                            t = q0 + qi
                            dst = scr[nm]["o"][t * 128:(t + 1) * 128, hc]
                            emit("pool", lambda e, dst=dst, o_=o_: e.dma_start(out=dst, in_=o_[:]),
                                 reads=[o_b], writes=[dbuf(nm, "o", t)], dma=True)
            return body

        def ph_outproj(nm, layer):
            S = dict(seqs)[nm]
            NT = S // 128 + 1
            wout = rwout if layer == 0 else dwout
            KO = 2 * KC

            def body(st):
                AT2 = tsb(st, "AT2", [128, KO, TB_OUT * 128], BF16)
                AT_b = [Buf() for _ in range(TB_OUT)]
                WT = TPool(st, "WT", [128, KC, 512], BF16, 2)
                orow = TPool(st, "orow", [128, 2 * D], BF16, 2)
                stg = TPool(st, "stg", [128, 512], F32, 3)
                for blk in range(0, NT, TB_OUT):
                    tiles = list(range(blk, min(NT, blk + TB_OUT)))
                    for ti, t in enumerate(tiles):
                        ot, ot_b = orow.get()
                        emit("sp", lambda e, ot=ot, t=t: e.dma_start(out=ot[:], in_=scr[nm]["o"][t * 128:(t + 1) * 128, 0:2 * D]),
                             reads=[dbuf(nm, "o", t)], writes=[ot_b], dma=True)
                        transposes([ot[:, c * 128:(c + 1) * 128] for c in range(KO)], [ot_b],
                                   lambda i0, n, ti=ti: AT2[:, i0:i0 + n, ti * 128:(ti + 1) * 128], [AT_b[ti]])
                    for cb in range(D // 512):
                        c0 = cb * 512
                        for half in range(2):
                            wt, wb = WT.get()
                            nparts = max(1, KC // 8)
                            per = KC // nparts
                            toks = {}
                            for part in range(nparts):
                                k0 = half * KC * 128 + part * per * 128
                                src = wout[k0:k0 + per * 128, c0:c0 + 512].rearrange("(a p) c -> p a c", p=128)
                                tmpb = Buf()
                                if part == 0:
                                    tmpb.w, tmpb.r = wb.w, wb.r
                                tok = emit("pool", lambda e, wt=wt, src=src, part=part, per=per: e.dma_start(
                                    out=wt[:, part * per:(part + 1) * per, :], in_=src), writes=[tmpb], dma=True)
                                toks[tok[0]] = (tok[1], tok[2])
                            wb.w, wb.r = toks, {}
                            for ti, t in enumerate(tiles):
                                for kc in range(KC):
                                    first = (half == 0 and kc == 0)
                                    last = (half == 1 and kc == KC - 1)
                                    emit("pe", lambda e, ti=ti, kc=kc, wt=wt, half=half, first=first, last=last: e.matmul(
                                        PSB[ti][:, :], AT2[:, half * KC + kc, ti * 128:(ti + 1) * 128], wt[:, kc, :],
                                        start=first, stop=last),
                                        reads=[AT_b[ti], wb], writes=[PSB_b[ti]], signal=(kc == KC - 1))
                        for ti, t in enumerate(tiles):
                            sg, sg_b = stg.get()
                            emit("act", lambda e, sg=sg, ti=ti: e.activation(out=sg[:], in_=PSB[ti][:, :], func=AF.Copy),
                                 reads=[PSB_b[ti]], writes=[sg_b])
                            dst = scr[nm]["m"][t * 128:(t + 1) * 128, c0:c0 + 512]
                            emit("pool", lambda e, dst=dst, sg=sg: e.dma_start(out=dst, in_=sg[:]),
                                 reads=[sg_b], writes=[dbuf(nm, "m", t)], dma=True)
            return body

        def ph_norm(nm, layer):
            S = dict(seqs)[nm]
            NT = S // 128 + 1

            def body(st):
                xrow = TPool(st, "xrow", [128, D], F32, 2)
                mrow = TPool(st, "mrow", [128, D], F32, 2)
                jrow = TPool(st, "jrow", [128, D], BF16, 2)
                stat = TPool(st, "stat", [128, 8], F32, 4)
                load_gain(post_n[layer:layer + 1, :])
                for t in range(NT):
                    if layer == 1 and t == 0:
                        continue
                    xt, xb = xrow.get()
                    load_x(nm, layer, t, xt, xb)
                    mt, mb = mrow.get()
                    emit("sp", lambda e, mt=mt, t=t: e.dma_start(out=mt[:], in_=scr[nm]["m"][t * 128:(t + 1) * 128, :]),
                         reads=[dbuf(nm, "m", t)], writes=[mb], dma=True)
                    jt, jb = jrow.get()
                    stt, st_b = rms_rstd(stat, jt[:], jb, mt[:], mb, D, 1e-6)
                    emit("dve", lambda e, mt=mt, stt=stt: e.scalar_tensor_tensor(
                        out=mt[:], in0=mt[:], scalar=stt[:, 3:4], in1=gain_t[:], op0=ALU.mult, op1=ALU.mult),
                        reads=[mb, st_b, gain_b], writes=[mb])
                    emit("dve", lambda e, mt=mt, xt=xt: e.tensor_tensor(out=xt[:], in0=xt[:], in1=mt[:], op=ALU.add),
                         reads=[mb, xb], writes=[xb])
                    if layer == 0:
                        dst = scr[nm]["x1"][t * 128:(t + 1) * 128, :]
                        wbuf = dbuf(nm, "x1", t)
                    else:
                        dst = yout[nm][(t - 1) * 128:t * 128, :]
                        wbuf = dbuf(nm, "y", t)
                    emit("pool", lambda e, dst=dst, xt=xt: e.dma_start(out=dst, in_=xt[:]),
                         reads=[xb], writes=[wbuf], dma=True)
            return body

        run_phase(ph_const)
        stop_after = cfg.get("stop_after")
        for nm, S in seqs:
            plist = [("in0", ph_inproj(nm, 0)), ("ret", ph_ret(nm)), ("out0", ph_outproj(nm, 0)), ("norm0", ph_norm(nm, 0)),
                     ("in1", ph_inproj(nm, 1)), ("att", ph_att(nm)), ("out1", ph_outproj(nm, 1)), ("norm1", ph_norm(nm, 1))]
            for pname, ph in plist:
                run_phase(ph)
                if stop_after == pname:
                    break
    return nc


def _rot_table(L, rot_dim, theta):
    half = rot_dim // 2
    inv_freq = np.power(np.float32(theta), -np.arange(half, dtype=np.float32) * np.float32(2.0) / np.float32(rot_dim)).astype(np.float32)
    pos = np.arange(L, dtype=np.float32)
    ang = (pos[:, None] * inv_freq[None, :]).astype(np.float32)
    return np.cos(ang).astype(np.float32), np.sin(ang).astype(np.float32)


def _const_tables(cfg):
    Lmax = max(cfg["S_S"], cfg["S_P"]) + 128
    tabs = {}
    for nm, rd, th in (("rtab", 256, RET_THETA), ("dtab", 32, ROPE_THETA)):
        c, s = _rot_table(Lmax, rd, th)
        t = np.zeros((Lmax, 2, rd // 2), np.float32)
        t[:, 0, :] = 1.0
        t[PADR:, 0, :] = c[:Lmax - PADR]
        t[PADR:, 1, :] = s[:Lmax - PADR]
        tabs[nm] = t
    j = np.arange(128, dtype=np.float32)[:, None]
    i = np.arange(128, dtype=np.float32)[None, :]
    ctab = np.zeros((128, 6, 128), np.float32)
    ctab[:, 0, :] = np.maximum(i - j, 0)
    ctab[:, 1, :] = (i >= j)
    ctab[:, 2, :] = np.maximum(j - i, 0)
    ctab[:, 3, :] = (j > i)
    ctab[:, 4, :] = i + 1 + 0 * j
    ctab[:, 5, :] = 128 - i + 0 * j
    ptab = np.zeros((128, 4), np.float32)
    ptab[:, 0] = 127 - np.arange(128)
    ptab[:, 1] = np.arange(128)
    ptab[:PADR, 2] = -30000.0
    tabs["ctab"] = ctab
    tabs["ptab"] = ptab
    tabs["ident"] = np.eye(128, dtype=np.float32)
    return tabs


_PROG_CACHE = {}


def run_cfg(cfg, inputs, debug=False):
    key = (tuple(sorted(cfg.items())), debug)
    if key not in _PROG_CACHE:
        _PROG_CACHE[key] = build_program(cfg, debug=debug)
    nc = _PROG_CACHE[key]
    f = lambda a: np.ascontiguousarray(np.asarray(a, dtype=np.float32))
    xp, xs = f(inputs["x_prompt"]), f(inputs["x_sample"])
    shared = dict(
        meta=f(inputs["meta_tokens"]), pre_norm=f(inputs["pre_norm"]), post_norm=f(inputs["post_norm"]),
        ret_w_in=f(inputs["ret_w_in"][0]), ret_w_out=f(inputs["ret_w_out"][0]),
        ret_decay=f(np.concatenate([np.asarray(inputs["ret_decay_fwd"]), np.asarray(inputs["ret_decay_bwd"])], 0)),
        diff_w_in=f(inputs["diff_w_in"][0]), diff_w_out=f(inputs["diff_w_out"][0]),
        diff_lam=f(np.concatenate([np.asarray(inputs[k]) for k in
                                   ("diff_lambda_q1", "diff_lambda_k1", "diff_lambda_q2", "diff_lambda_k2")], 0)),
        diff_subln=f(inputs["diff_subln"]),
    )
    shared.update(_const_tables(cfg))
    in_maps = []
    for c in range(8):
        m = dict(shared)
        m["x_p"] = xp[c]
        m["x_s"] = xs[c % xs.shape[0]]
        in_maps.append(m)
    res = run_bass_kernel_spmd(nc, in_maps, core_ids=list(range(8)))
    y_p = np.stack([res.results[c]["y_p"] for c in range(8)], 0)
    y_s = np.stack([res.results[c]["y_s"] for c in range(xs.shape[0])], 0)
    return (y_p, y_s), res


def kernel(**inputs):
    (y_p, y_s), _ = run_cfg(FULL, inputs)
    return (y_p.astype(np.float32), y_s.astype(np.float32))
```

```python
import math
from contextlib import ExitStack
import numpy as np
import concourse.bass as bass
import concourse.mybir as mybir
from concourse.bass_utils import run_bass_kernel_spmd

F32 = mybir.dt.float32
BF16 = mybir.dt.bfloat16
AF = mybir.ActivationFunctionType
ALU = mybir.AluOpType
AX = mybir.AxisListType

FULL = dict(D=4096, HR=16, HD=32, S_P=2048, S_S=8192)
N_META = 16
PADR = 112
RET_THETA = 10000.0
ROPE_THETA = 500000.0
EPOCH = 28000


class Buf:
    __slots__ = ("w", "r")

    def __init__(self):
        self.w = {}
        self.r = {}


class _Eng:
    def __init__(self, name):
        self.name = name
        self.ops = []
        self.sem = None
        self.epoch = 0
        self.count = 0
        self.pending = False
        self.known = {}
        self.slots = []
        self.slot_i = 0
        self.last = None


class Sched:
    def __init__(self, nc, stack, n_slots):
        self.nc = nc
        self.stack = stack
        self.E = {n: _Eng(n) for n in ("pe", "act", "dve", "pool", "sp")}
        self.nsem = 0
        for n, e in self.E.items():
            e.sem = self._newsem(n)
        for n, k in n_slots.items():
            for i in range(k):
                self.E[n].slots.append([self._newsem(f"{n}d{i}"), 0, (n, "slot", i)])

    def _newsem(self, name):
        self.nsem += 1
        return self.stack.enter_context(self.nc.semaphore(f"s{self.nsem}_{name}"))

    def _need(self, eng, waits, key, sem, val):
        if val <= 0:
            return
        if eng.name == "pe" and key[0] == "pe" and key[1] != "slot":
            return
        if key[0] == eng.name and key[1] != "slot":
            if key[1] != eng.epoch or eng.count - val >= 2:
                return
        if eng.known.get(key, 0) >= val:
            return
        cur = waits.get(key)
        if cur is None or cur[1] < val:
            waits[key] = (sem, val)

    def emit(self, en, fn, reads=(), writes=(), dma=False, signal=True):
        eng = self.E[en]
        waits = {}
        for b in reads:
            for k, (s, v) in b.w.items():
                self._need(eng, waits, k, s, v)
        for b in writes:
            for k, (s, v) in b.w.items():
                self._need(eng, waits, k, s, v)
            for k, (s, v) in b.r.items():
                self._need(eng, waits, k, s, v)
        if dma:
            slot = eng.slots[eng.slot_i]
            eng.slot_i = (eng.slot_i + 1) % len(eng.slots)
            self._need(eng, waits, slot[2], slot[0], slot[1])
            slot[1] += 16
            tok = (slot[2], slot[0], slot[1])
            inc = (slot[0], 16)
        else:
            if eng.count >= EPOCH and not eng.pending:
                eng.epoch += 1
                eng.count = 0
                eng.sem = self._newsem(f"{en}e{eng.epoch}")
            key = (en, eng.epoch)
            if signal:
                eng.count += 1
                eng.pending = False
                tok = (key, eng.sem, eng.count)
                inc = (eng.sem, 1)
            else:
                eng.pending = True
                tok = (key, eng.sem, eng.count + 1)
                inc = None
            eng.last = tok
        for k, (s, v) in waits.items():
            eng.known[k] = v
        eng.ops.append((list(waits.values()), fn, inc))
        k, s, v = tok
        for b in reads:
            cur = b.r.get(k)
            if cur is None or cur[1] < v:
                b.r[k] = (s, v)
        for b in writes:
            b.w = {k: (s, v)}
            b.r = {}
        return tok

    def barrier(self):
        toks = []
        for e in self.E.values():
            assert not e.pending
            if e.last is not None:
                toks.append(e.last)
            for slot in e.slots:
                if slot[1] > 0:
                    toks.append((slot[2], slot[0], slot[1]))
        for e in self.E.values():
            waits = {}
            for k, s, v in toks:
                if k[0] == e.name and k[1] != "slot":
                    continue
                if e.known.get(k, 0) >= v:
                    continue
                waits[k] = (s, v)
                e.known[k] = v
            if waits:
                e.ops.append((list(waits.values()), None, None))

    def flush(self, block):
        def mk(en):
            def run(engobj):
                for waits, fn, inc in self.E[en].ops:
                    for s, v in waits:
                        engobj.wait_ge(s, v)
                    if fn is None:
                        continue
                    ins = fn(engobj)
                    if inc is not None:
                        ins.then_inc(inc[0], inc[1])
                self.E[en].ops = []
            return run
        block.tensor(mk("pe"))
        block.scalar(mk("act"))
        block.vector(mk("dve"))
        block.gpsimd(mk("pool"))
        block.sync(mk("sp"))


def build_program(cfg, debug=False):
    D = cfg["D"]
    HR = cfg["HR"]
    HD = cfg["HD"]
    KC = D // 128
    RQK = HR * 256
    RV = HR * 512
    DQK = HD * 256
    DV = HD * 256
    assert RV == 2 * D and DV == 2 * D
    NR_IN = 2 * RQK + 2 * RV
    ND_IN = 2 * DQK + 2 * DV
    lam_init = 0.8 - 0.6 * math.exp(-0.3 * 1)
    seqs = [("s", cfg["S_S"]), ("p", cfg["S_P"])]
    TB_IN = cfg.get("TB_IN", 8)
    TB_OUT = cfg.get("TB_OUT", 4)

    nc = bass.Bass("TRN2", target_bir_lowering=False)
    okind = "ExternalOutput" if debug else "Internal"

    def dram(name, shape, dt, kind):
        return nc.dram_tensor(name, list(shape), dt, kind=kind).ap()

    xin, yout = {}, {}
    for nm, S in seqs:
        xin[nm] = dram(f"x_{nm}", [S, D], F32, "ExternalInput")
        yout[nm] = dram(f"y_{nm}", [S, D], F32, "ExternalOutput")
    meta = dram("meta", [N_META, D], F32, "ExternalInput")
    pre_n = dram("pre_norm", [2, D], F32, "ExternalInput")
    post_n = dram("post_norm", [2, D], F32, "ExternalInput")
    rwin = dram("ret_w_in", [D, NR_IN], F32, "ExternalInput")
    rwout = dram("ret_w_out", [RV, D], F32, "ExternalInput")
    rdec = dram("ret_decay", [2, HR], F32, "ExternalInput")
    dwin = dram("diff_w_in", [D, ND_IN], F32, "ExternalInput")
    dwout = dram("diff_w_out", [DV, D], F32, "ExternalInput")
    dlam = dram("diff_lam", [4, 128], F32, "ExternalInput")
    dsub = dram("diff_subln", [1, 256], F32, "ExternalInput")
    Lmax = max(S for _, S in seqs) + 128
    rtab = dram("rtab", [Lmax, 2, 128], F32, "ExternalInput")
    dtab = dram("dtab", [Lmax, 2, 16], F32, "ExternalInput")
    ctab = dram("ctab", [128, 6, 128], F32, "ExternalInput")
    ptab = dram("ptab", [128, 4], F32, "ExternalInput")
    ident_in = dram("ident", [128, 128], F32, "ExternalInput")

    scr = {}
    dbufs = {}
    for nm, S in seqs:
        Lp = S + 128
        scr[nm] = dict(
            q=dram(f"q_{nm}", [Lp, 2 * D], BF16, okind),
            k=dram(f"k_{nm}", [Lp, 2 * D], BF16, okind),
            v=dram(f"v_{nm}", [Lp, 2 * D], BF16, okind),
            g=dram(f"g_{nm}", [Lp, 2 * D], BF16, okind),
            o=dram(f"o_{nm}", [Lp, 2 * D], BF16, okind),
            m=dram(f"m_{nm}", [Lp, D], F32, okind),
            x1=dram(f"x1_{nm}", [Lp, D], F32, okind),
        )

    def dbuf(*key):
        b = dbufs.get(key)
        if b is None:
            b = dbufs[key] = Buf()
        return b

    top = ExitStack()
    with top:
        sch = Sched(nc, top, {"sp": 16, "pool": 12})
        emit = sch.emit

        uid = [0]

        def tsb(stack, name, shape, dt):
            uid[0] += 1
            return stack.enter_context(nc.sbuf_tensor(f"sb{uid[0]}_{name}", list(shape), dt))

        ident = tsb(top, "ident", [128, 128], BF16)
        ctab_s = tsb(top, "ctab_s", [128, 6, 128], F32)
        ptab_s = tsb(top, "ptab_s", [128, 4], F32)
        subln_s = tsb(top, "subln_s", [128, 256], F32)
        lg = tsb(top, "lg", [128, 2, HR], F32)
        cdec = tsb(top, "cdec", [128, 2, HR], F32)
        lam_t = tsb(top, "lam_t", [128, 8], F32)
        zero_c = tsb(top, "zero_c", [128, 1], F32)
        gain_t = tsb(top, "gain_t", [128, D], F32)
        gain_b = Buf()
        CB = Buf()
        CONST = [CB]

        PSB = [top.enter_context(nc.psum_tensor(f"psb{i}", [128, 512], F32)) for i in range(6)]
        PST = [top.enter_context(nc.psum_tensor(f"pst{i}", [128, 8, 128], BF16)) for i in range(2)]
        PSB_b = [Buf() for _ in range(6)]
        PST_b = [Buf() for _ in range(2)]
        bank_i = [0]
        tp_i = [0]

        def nextbank():
            i = bank_i[0]
            bank_i[0] = (i + 1) % 6
            return i

        class TPool:
            def __init__(self, stack, name, shape, dt, n):
                self.t = [tsb(stack, f"{name}{i}", shape, dt) for i in range(n)]
                self.b = [Buf() for _ in range(n)]
                self.i = 0

            def get(self):
                i = self.i
                self.i = (i + 1) % len(self.t)
                return self.t[i], self.b[i]

        def run_phase(fn):
            with ExitStack() as st, nc.Block() as block:
                fn(st)
                sch.barrier()
                sch.flush(block)

        def ph_const(st):
            ident_f = tsb(st, "ident_f", [128, 128], F32)
            dec_raw = tsb(st, "dec_raw", [128, 2, HR], F32)
            lam_in = tsb(st, "lam_in", [128, 4, 128], F32)
            B0 = Buf()
            emit("sp", lambda e: e.dma_start(out=ident_f[:], in_=ident_in[:, :]), writes=[B0], dma=True)
            emit("sp", lambda e: e.dma_start(out=ctab_s[:], in_=ctab[:, :, :]), writes=[CB], dma=True)
            emit("sp", lambda e: e.dma_start(out=ptab_s[:], in_=ptab[:, :]), writes=[CB], dma=True)
            emit("sp", lambda e: e.dma_start(out=subln_s[:], in_=dsub[0:1, :].partition_broadcast(128)), writes=[CB], dma=True)
            for i in range(2):
                emit("sp", lambda e, i=i: e.dma_start(out=dec_raw[:, i, :], in_=rdec[i:i + 1, :].partition_broadcast(128)),
                     writes=[B0], dma=True)
            for i in range(4):
                emit("sp", lambda e, i=i: e.dma_start(out=lam_in[:, i, :], in_=dlam[i:i + 1, :].partition_broadcast(128)),
                     writes=[B0], dma=True)
            emit("dve", lambda e: e.tensor_copy(out=ident[:], in_=ident_f[:]), reads=[B0], writes=[CB])
            emit("dve", lambda e: e.memset(zero_c[:], 0.0), writes=[CB])
            emit("act", lambda e: e.activation(out=lg[:], in_=dec_raw[:], func=AF.Exp), reads=[B0], writes=[CB])
            emit("dve", lambda e: e.tensor_scalar(out=lg[:], in0=lg[:], scalar1=-1.0, scalar2=None, op0=ALU.mult),
                 reads=[CB], writes=[CB])
            emit("act", lambda e: e.activation(out=cdec[:], in_=lg[:], func=AF.Exp, scale=128.0), reads=[CB], writes=[CB])
            emit("dve", lambda e: e.tensor_tensor(out=lam_in[:, 0, :], in0=lam_in[:, 0, :], in1=lam_in[:, 1, :], op=ALU.mult),
                 reads=[B0], writes=[B0])
            emit("dve", lambda e: e.tensor_tensor(out=lam_in[:, 2, :], in0=lam_in[:, 2, :], in1=lam_in[:, 3, :], op=ALU.mult),
                 reads=[B0], writes=[B0])
            emit("dve", lambda e: e.reduce_sum(out=lam_t[:, 0:1], in_=lam_in[:, 0, :], axis=AX.X), reads=[B0], writes=[CB])
            emit("dve", lambda e: e.reduce_sum(out=lam_t[:, 1:2], in_=lam_in[:, 2, :], axis=AX.X), reads=[B0], writes=[CB])
            emit("act", lambda e: e.activation(out=lam_t[:, 2:4], in_=lam_t[:, 0:2], func=AF.Exp), reads=[CB], writes=[CB])
            emit("dve", lambda e: e.tensor_tensor(out=lam_t[:, 4:5], in0=lam_t[:, 3:4], in1=lam_t[:, 2:3], op=ALU.subtract),
                 reads=[CB], writes=[CB])
            emit("dve", lambda e: e.tensor_scalar(out=lam_t[:, 4:5], in0=lam_t[:, 4:5], scalar1=-lam_init, scalar2=None,
                                                  op0=ALU.add), reads=[CB], writes=[CB])

        def load_gain(row_ap):
            emit("sp", lambda e: e.dma_start(out=gain_t[:], in_=row_ap.partition_broadcast(128)), writes=[gain_b], dma=True)

        def rms_rstd(stat, junk_ap, junk_b, src_ap, src_b, ncols, eps):
            stt, st_b = stat.get()
            emit("act", lambda e: e.activation(out=junk_ap, in_=src_ap, func=AF.Square, accum_out=stt[:, 0:1]),
                 reads=[src_b], writes=[junk_b, st_b])
            emit("dve", lambda e: e.tensor_scalar(out=stt[:, 1:2], in0=stt[:, 0:1], scalar1=1.0 / ncols, scalar2=eps,
                                                  op0=ALU.mult, op1=ALU.add), reads=[st_b], writes=[st_b])
            emit("act", lambda e: e.activation(out=stt[:, 2:3], in_=stt[:, 1:2], func=AF.Sqrt), reads=[st_b], writes=[st_b])
            emit("dve", lambda e: e.reciprocal(out=stt[:, 3:4], in_=stt[:, 2:3]), reads=[st_b], writes=[st_b])
            return stt, st_b

        def transposes(src_blocks, src_b, dst_fn, dst_bufs, evac="dve"):
            i0 = 0
            nblk = len(src_blocks)
            while i0 < nblk:
                n = min(8, nblk - i0)
                bk = tp_i[0] % 2
                tp_i[0] += 1
                ptv = PST[bk]
                for j in range(n):
                    src = src_blocks[i0 + j]
                    emit("pe", lambda e, j=j, src=src, ptv=ptv: e.transpose(ptv[:, j, :], src, ident[:]),
                         reads=[*src_b, *CONST], writes=[PST_b[bk]], signal=(j == n - 1))
                dst = dst_fn(i0, n)
                if evac == "act":
                    emit(evac, lambda e, dst=dst, ptv=ptv, n=n: e.copy(out=dst, in_=ptv[:, 0:n, :]),
                         reads=[PST_b[bk]], writes=dst_bufs)
                else:
                    emit(evac, lambda e, dst=dst, ptv=ptv, n=n: e.tensor_copy(out=dst, in_=ptv[:, 0:n, :]),
                         reads=[PST_b[bk]], writes=dst_bufs)
                i0 += n

        def x_rows(nm, layer, t):
            if layer == 0:
                return xin[nm][(t - 1) * 128:t * 128, :] if t > 0 else None
            return scr[nm]["x1"][t * 128:(t + 1) * 128, :]

        def load_x(nm, layer, t, tl, tb):
            if layer == 0 and t == 0:
                emit("dve", lambda e: e.memset(tl[:], 0.0), writes=[tb])
                emit("sp", lambda e: e.dma_start(out=tl[PADR:128, :], in_=meta[:, :]), writes=[tb], dma=True)
            else:
                src = x_rows(nm, layer, t)
                rb = [dbuf(nm, "x1", t)] if layer == 1 else []
                emit("sp", lambda e: e.dma_start(out=tl[:], in_=src), reads=rb, writes=[tb], dma=True)

        def ph_inproj(nm, layer):
            S = dict(seqs)[nm]
            NT = S // 128 + 1
            win = rwin if layer == 0 else dwin
            QK = RQK if layer == 0 else DQK
            VV = RV if layer == 0 else DV
            NIN = 2 * QK + 2 * VV

            def body(st):
                AT = tsb(st, "AT", [128, KC, TB_IN * 128], BF16)
                AT_b = [Buf() for _ in range(TB_IN)]
                WT = TPool(st, "WT", [128, KC, 512], BF16, 2)
                xrow = TPool(st, "xrow", [128, D], F32, 1)
                hrow = TPool(st, "hrow", [128, D], BF16, 2)
                stat = TPool(st, "stat", [128, 8], F32, 4)
                stg = TPool(st, "stg", [128, 512], F32, 3)
                obf = TPool(st, "obf", [128, 512], BF16, 4)
                rot = TPool(st, "rot", [128, 4, 256], F32, 2)
                tw = 128 if layer == 0 else 16
                tabs = tsb(st, "tabs", [128, TB_IN, 2, tw], F32)
                tabs_b = Buf()
                tabd = rtab if layer == 0 else dtab
                load_gain(pre_n[layer:layer + 1, :])
                for blk in range(0, NT, TB_IN):
                    tiles = list(range(blk, min(NT, blk + TB_IN)))
                    nt = len(tiles)
                    emit("sp", lambda e, blk=blk, nt=nt: e.dma_start(
                        out=tabs[:, 0:nt, :, :],
                        in_=tabd[blk * 128:(blk + nt) * 128, :, :].rearrange("(t p) a c -> p t a c", p=128)),
                        writes=[tabs_b], dma=True)
                    for ti, t in enumerate(tiles):
                        xt, xb = xrow.get()
                        load_x(nm, layer, t, xt, xb)
                        ht, hb = hrow.get()
                        stt, st_b = rms_rstd(stat, ht[:], hb, xt[:], xb, D, 1e-6)
                        emit("dve", lambda e, ht=ht, xt=xt, stt=stt: e.scalar_tensor_tensor(
                            out=ht[:], in0=xt[:], scalar=stt[:, 3:4], in1=gain_t[:], op0=ALU.mult, op1=ALU.mult),
                            reads=[xb, st_b, gain_b], writes=[hb])
                        blocks = [ht[:, c * 128:(c + 1) * 128] for c in range(KC)]
                        transposes(blocks, [hb],
                                   lambda i0, n, ti=ti: AT[:, i0:i0 + n, ti * 128:(ti + 1) * 128], [AT_b[ti]])
                    for cb in range(NIN // 512):
                        c0 = cb * 512
                        if c0 < QK:
                            sec, sc0 = "q", c0
                        elif c0 < 2 * QK:
                            sec, sc0 = "k", c0 - QK
                        elif c0 < 2 * QK + VV:
                            sec, sc0 = "v", c0 - 2 * QK
                        else:
                            sec, sc0 = "g", c0 - 2 * QK - VV
                        wt, wb = WT.get()
                        nparts = max(1, KC // 8)
                        per = KC // nparts
                        toks = {}
                        for part in range(nparts):
                            src = win[part * per * 128:(part + 1) * per * 128, c0:c0 + 512].rearrange(
                                "(a p) c -> p a c", p=128)
                            tmpb = Buf()
                            if part == 0:
                                tmpb.w, tmpb.r = wb.w, wb.r
                            tok = emit("pool", lambda e, wt=wt, src=src, part=part, per=per: e.dma_start(
                                out=wt[:, part * per:(part + 1) * per, :], in_=src), writes=[tmpb], dma=True)
                            toks[tok[0]] = (tok[1], tok[2])
                        wb.w, wb.r = toks, {}
                        for ti, t in enumerate(tiles):
                            bk = nextbank()
                            for kc in range(KC):
                                emit("pe", lambda e, bk=bk, kc=kc, ti=ti, wt=wt: e.matmul(
                                    PSB[bk][:, :], AT[:, kc, ti * 128:(ti + 1) * 128], wt[:, kc, :],
                                    start=(kc == 0), stop=(kc == KC - 1)),
                                    reads=[AT_b[ti], wb], writes=[PSB_b[bk]], signal=(kc == KC - 1))
                            ob, ob_b = obf.get()
                            if sec == "v":
                                emit("dve", lambda e, ob=ob, bk=bk: e.tensor_copy(out=ob[:], in_=PSB[bk][:, :]),
                                     reads=[PSB_b[bk]], writes=[ob_b])
                            elif sec == "g":
                                emit("act", lambda e, ob=ob, bk=bk: e.activation(out=ob[:], in_=PSB[bk][:, :], func=AF.Silu),
                                     reads=[PSB_b[bk]], writes=[ob_b])
                            else:
                                sg, sg_b = stg.get()
                                scl = (1.0 / 16.0) if (layer == 0 and sec == "q") else 1.0
                                emit("act", lambda e, sg=sg, bk=bk, scl=scl: e.activation(
                                    out=sg[:], in_=PSB[bk][:, :], func=AF.Copy, scale=scl),
                                    reads=[PSB_b[bk]], writes=[sg_b])
                                if layer == 1:
                                    emit("dve", lambda e, ob=ob, sg=sg: e.tensor_copy(out=ob[:], in_=sg[:]),
                                         reads=[sg_b], writes=[ob_b])
                                ngrp = 2 if layer == 0 else 4
                                gw = 512 // ngrp
                                hw = 128 if layer == 0 else 16
                                r, r_b = rot.get()
                                sgv = sg[:].rearrange("p (g w) -> p g w", w=gw)
                                obv = ob[:].rearrange("p (g w) -> p g w", w=gw)
                                x1 = sgv[:, :, 0:hw]
                                x2 = sgv[:, :, hw:2 * hw]
                                o1 = obv[:, :, 0:hw]
                                o2 = obv[:, :, hw:2 * hw]
                                cs = tabs[:, ti, 0:1, :].broadcast_to([128, ngrp, hw])
                                sn = tabs[:, ti, 1:2, :].broadcast_to([128, ngrp, hw])
                                T = [r[:, i, 0:ngrp * hw].rearrange("p (g w) -> p g w", w=hw) for i in range(4)]
                                rd = [sg_b, tabs_b, r_b]
                                emit("dve", lambda e, T=T, x1=x1, cs=cs: e.tensor_tensor(out=T[0], in0=x1, in1=cs, op=ALU.mult),
                                     reads=rd, writes=[r_b])
                                emit("dve", lambda e, T=T, x2=x2, sn=sn: e.tensor_tensor(out=T[1], in0=x2, in1=sn, op=ALU.mult),
                                     reads=rd, writes=[r_b])
                                emit("dve", lambda e, T=T, x2=x2, cs=cs: e.tensor_tensor(out=T[2], in0=x2, in1=cs, op=ALU.mult),
                                     reads=rd, writes=[r_b])
                                emit("dve", lambda e, T=T, x1=x1, sn=sn: e.tensor_tensor(out=T[3], in0=x1, in1=sn, op=ALU.mult),
                                     reads=rd, writes=[r_b])
                                emit("dve", lambda e, T=T, o1=o1: e.tensor_tensor(out=o1, in0=T[0], in1=T[1], op=ALU.subtract),
                                     reads=[r_b], writes=[ob_b])
                                emit("dve", lambda e, T=T, o2=o2: e.tensor_tensor(out=o2, in0=T[2], in1=T[3], op=ALU.add),
                                     reads=[r_b], writes=[ob_b])
                            dst = scr[nm][sec][t * 128:(t + 1) * 128, sc0:sc0 + 512]
                            emit("pool", lambda e, dst=dst, ob=ob: e.dma_start(out=dst, in_=ob[:]),
                                 reads=[ob_b], writes=[dbuf(nm, sec, t)], dma=True)
            return body

        def ph_ret(nm):
            S = dict(seqs)[nm]
            NT = S // 128 + 1

            def body(st):
                oacc = tsb(st, "oacc", [128, NT, 512], F32)
                oacc_b = [Buf() for _ in range(NT)]
                Sst = [tsb(st, f"Sst{d}", [128, 2, 512], F32) for d in range(2)]
                Sbf = [tsb(st, f"Sbf{d}", [128, 2, 512], BF16) for d in range(2)]
                Sst_b = [Buf(), Buf()]
                Sbf_b = [Buf(), Buf()]
                htab = tsb(st, "htab", [128, 5, 128], F32)
                kd = tsb(st, "kd", [128, 2], F32)
                htab_b = Buf()
                qkv = TPool(st, "qkv", [128, 1024], BF16, 3)
                qT = TPool(st, "qT", [128, 4, 128], BF16, 3)
                qTd = TPool(st, "qTd", [128, 2, 128], BF16, 3)
                kdt = TPool(st, "kdt", [128, 256], BF16, 3)
                SD = TPool(st, "SD", [128, 128], BF16, 3)
                gl = TPool(st, "gl", [128, 512], BF16, 2)
                ob = TPool(st, "ob", [128, 512], BF16, 2)
                junk = TPool(st, "junk", [128, 512], F32, 2)
                stat = TPool(st, "stat", [128, 8], F32, 4)
                for h in range(HR):
                    lf = lg[:, 0, h:h + 1]
                    lb = lg[:, 1, h:h + 1]
                    emit("act", lambda e, lf=lf: e.activation(out=htab[:, 0, :], in_=ctab_s[:, 0, :], func=AF.Exp, scale=lf),
                         reads=CONST, writes=[htab_b])
                    emit("act", lambda e, lb=lb: e.activation(out=htab[:, 1, :], in_=ctab_s[:, 2, :], func=AF.Exp, scale=lb),
                         reads=CONST, writes=[htab_b])
                    emit("act", lambda e, lf=lf: e.activation(out=htab[:, 3, :], in_=ctab_s[:, 4, :], func=AF.Exp, scale=lf),
                         reads=CONST, writes=[htab_b])
                    emit("act", lambda e, lb=lb: e.activation(out=htab[:, 4, :], in_=ctab_s[:, 5, :], func=AF.Exp, scale=lb),
                         reads=CONST, writes=[htab_b])
                    emit("act", lambda e, lf=lf: e.activation(out=kd[:, 0:1], in_=ptab_s[:, 0:1], func=AF.Exp, scale=lf),
                         reads=CONST, writes=[htab_b])
                    emit("act", lambda e, lb=lb: e.activation(out=kd[:, 1:2], in_=ptab_s[:, 1:2], func=AF.Exp, scale=lb),
                         reads=CONST, writes=[htab_b])
                    emit("dve", lambda e: e.tensor_tensor(out=htab[:, 0, :], in0=htab[:, 0, :], in1=ctab_s[:, 1, :], op=ALU.mult),
                         reads=[htab_b, *CONST], writes=[htab_b])
                    emit("dve", lambda e: e.tensor_tensor(out=htab[:, 1, :], in0=htab[:, 1, :], in1=ctab_s[:, 3, :], op=ALU.mult),
                         reads=[htab_b, *CONST], writes=[htab_b])
                    emit("dve", lambda e: e.tensor_tensor(out=htab[:, 2, :], in0=htab[:, 0, :], in1=htab[:, 1, :], op=ALU.add),
                         reads=[htab_b], writes=[htab_b])
                    for d in range(2):
                        emit("dve", lambda e, d=d: e.memset(Sst[d][:], 0.0), writes=[Sst_b[d]])
                        emit("dve", lambda e, d=d: e.memset(Sbf[d][:], 0.0), writes=[Sbf_b[d]])
                    for d in range(2):
                        order = range(NT) if d == 0 else range(NT - 1, -1, -1)
                        for n in order:
                            tl, tl_b = qkv.get()
                            r0 = n * 128
                            emit("sp", lambda e, tl=tl, r0=r0, h=h: e.dma_start(
                                out=tl[:, 0:256], in_=scr[nm]["q"][r0:r0 + 128, h * 256:(h + 1) * 256]),
                                reads=[dbuf(nm, "q", n)], writes=[tl_b], dma=True)
                            emit("sp", lambda e, tl=tl, r0=r0, h=h: e.dma_start(
                                out=tl[:, 256:512], in_=scr[nm]["k"][r0:r0 + 128, h * 256:(h + 1) * 256]),
                                reads=[dbuf(nm, "k", n)], writes=[tl_b], dma=True)
                            emit("sp", lambda e, tl=tl, r0=r0, h=h: e.dma_start(
                                out=tl[:, 512:1024], in_=scr[nm]["v"][r0:r0 + 128, h * 512:(h + 1) * 512]),
                                reads=[dbuf(nm, "v", n)], writes=[tl_b], dma=True)
                            qt, qt_b = qT.get()
                            nb = 4 if d == 0 else 2
                            transposes([tl[:, c * 128:(c + 1) * 128] for c in range(nb)], [tl_b],
                                       lambda i0, n_, qt=qt: qt[:, i0:i0 + n_, :], [qt_b], evac="act")
                            qd, qd_b = qTd.get()
                            Rt = htab[:, 3 + d, :]
                            for c in range(2):
                                emit("dve", lambda e, qd=qd, qt=qt, c=c, Rt=Rt: e.tensor_tensor(
                                    out=qd[:, c, :], in0=qt[:, c, :], in1=Rt, op=ALU.mult),
                                    reads=[qt_b, htab_b], writes=[qd_b])
                            kt_, kt_b = kdt.get()
                            emit("dve", lambda e, kt_=kt_, tl=tl, d=d: e.tensor_scalar(
                                out=kt_[:], in0=tl[:, 256:512], scalar1=kd[:, d:d + 1], scalar2=None, op0=ALU.mult),
                                reads=[tl_b, htab_b], writes=[kt_b])
                            bo = nextbank()
                            if d == 0:
                                bs = nextbank()
                                for c in range(2):
                                    emit("pe", lambda e, bs=bs, qt=qt, c=c: e.matmul(
                                        PSB[bs][:, 0:128], qt[:, 2 + c, :], qt[:, c, :], start=(c == 0), stop=(c == 1)),
                                        reads=[qt_b], writes=[PSB_b[bs]], signal=(c == 1))
                                sd, sd_b = SD.get()
                                emit("dve", lambda e, sd=sd, bs=bs: e.tensor_tensor(
                                    out=sd[:], in0=PSB[bs][:, 0:128], in1=htab[:, 2, :], op=ALU.mult),
                                    reads=[PSB_b[bs], htab_b], writes=[sd_b])
                                emit("pe", lambda e, bo=bo, sd=sd, tl=tl: e.matmul(
                                    PSB[bo][:, :], sd[:], tl[:, 512:1024], start=True, stop=False),
                                    reads=[sd_b, tl_b], writes=[PSB_b[bo]], signal=False)
                            for c in range(2):
                                emit("pe", lambda e, bo=bo, qd=qd, c=c, d=d: e.matmul(
                                    PSB[bo][:, :], qd[:, c, :], Sbf[d][:, c, :], start=(d == 1 and c == 0), stop=(c == 1)),
                                    reads=[qd_b, Sbf_b[d]], writes=[PSB_b[bo]], signal=(c == 1))
                            if d == 0:
                                emit("act", lambda e, n=n, bo=bo: e.activation(out=oacc[:, n, :], in_=PSB[bo][:, :], func=AF.Copy),
                                     reads=[PSB_b[bo]], writes=[oacc_b[n]])
                            else:
                                emit("dve", lambda e, n=n, bo=bo: e.tensor_tensor(
                                    out=oacc[:, n, :], in0=oacc[:, n, :], in1=PSB[bo][:, :], op=ALU.add),
                                    reads=[PSB_b[bo], oacc_b[n]], writes=[oacc_b[n]])
                            for c in range(2):
                                bu = nextbank()
                                emit("pe", lambda e, bu=bu, kt_=kt_, tl=tl, c=c: e.matmul(
                                    PSB[bu][:, :], kt_[:, c * 128:(c + 1) * 128], tl[:, 512:1024], start=True, stop=True),
                                    reads=[kt_b, tl_b], writes=[PSB_b[bu]])
                                emit("dve", lambda e, bu=bu, c=c, d=d, h=h: e.scalar_tensor_tensor(
                                    out=Sst[d][:, c, :], in0=Sst[d][:, c, :], scalar=cdec[:, d, h:h + 1], in1=PSB[bu][:, :],
                                    op0=ALU.mult, op1=ALU.add),
                                    reads=[PSB_b[bu], Sst_b[d], *CONST], writes=[Sst_b[d]])
                            emit("act", lambda e, d=d: e.activation(out=Sbf[d][:], in_=Sst[d][:], func=AF.Copy),
                                 reads=[Sst_b[d]], writes=[Sbf_b[d]])
                    for n in range(NT):
                        g_, g_b = gl.get()
                        r0 = n * 128
                        emit("sp", lambda e, g_=g_, r0=r0, h=h: e.dma_start(
                            out=g_[:], in_=scr[nm]["g"][r0:r0 + 128, h * 512:(h + 1) * 512]),
                            reads=[dbuf(nm, "g", n)], writes=[g_b], dma=True)
                        jk, jk_b = junk.get()
                        stt, st_b = rms_rstd(stat, jk[:], jk_b, oacc[:, n, :], oacc_b[n], 512, 1e-6)
                        o_, o_b = ob.get()
                        emit("dve", lambda e, o_=o_, n=n, stt=stt, g_=g_: e.scalar_tensor_tensor(
                            out=o_[:], in0=oacc[:, n, :], scalar=stt[:, 3:4], in1=g_[:], op0=ALU.mult, op1=ALU.mult),
                            reads=[oacc_b[n], st_b, g_b], writes=[o_b])
                        dst = scr[nm]["o"][r0:r0 + 128, h * 512:(h + 1) * 512]
                        emit("pool", lambda e, dst=dst, o_=o_: e.dma_start(out=dst, in_=o_[:]),
                             reads=[o_b], writes=[dbuf(nm, "o", n)], dma=True)
            return body

        def ph_att(nm):
            S = dict(seqs)[nm]
            NT = S // 128 + 1
            Lp = NT * 128
            scale = 128.0 ** -0.5

            def body(st):
                kload = tsb(st, "kload", [128, NT, 256], BF16)
                NCH = (NT + 7) // 8
                kload_b = [Buf() for _ in range(NCH)]
                KT = tsb(st, "KT", [128, 2, Lp], BF16)
                KT_b = Buf()
                Vx = tsb(st, "Vx", [128, NT, 257], BF16)
                Vx_b = [Buf() for _ in range(NCH)]
                qload = TPool(st, "qload", [128, 4, 256], BF16, 2)
                QT = TPool(st, "QT", [128, 2, 512], BF16, 2)
                gl = TPool(st, "gl", [128, 4, 256], BF16, 2)
                Pt = TPool(st, "Pt", [128, 512], BF16, 4)
                o0 = TPool(st, "o0", [128, 4, 256], F32, 2)
                tmpo = TPool(st, "tmpo", [128, 256], F32, 2)
                ob = TPool(st, "ob", [128, 256], BF16, 3)
                junk = TPool(st, "junk", [128, 256], F32, 2)
                stat = TPool(st, "stat", [128, 8], F32, 6)
                emit("dve", lambda e: e.memset(Vx[:, :, 256:257], 1.0), writes=Vx_b)
                sbank = [0]
                for h in range(HD):
                    hc = slice(h * 256, (h + 1) * 256)
                    rdk = [dbuf(nm, "k", t) for t in range(NT)]
                    rdv = [dbuf(nm, "v", t) for t in range(NT)]
                    for ch in range(NCH):
                        ta, tb_ = ch * 8, min(NT, ch * 8 + 8)
                        emit("sp", lambda e, hc=hc, ta=ta, tb_=tb_: e.dma_start(
                            out=kload[:, ta:tb_, :], in_=scr[nm]["k"][ta * 128:tb_ * 128, hc].rearrange("(t p) c -> p t c", p=128)),
                            reads=rdk[ta:tb_], writes=[kload_b[ch]], dma=True)
                        emit("sp", lambda e, hc=hc, ta=ta, tb_=tb_: e.dma_start(
                            out=Vx[:, ta:tb_, 0:256], in_=scr[nm]["v"][ta * 128:tb_ * 128, hc].rearrange("(t p) c -> p t c", p=128)),
                            reads=rdv[ta:tb_], writes=[Vx_b[ch]], dma=True)
                    for t0 in range(0, NT, 4):
                        n4 = min(4, NT - t0)
                        blocks = [kload[:, t0 + tt, m * 128:(m + 1) * 128] for tt in range(n4) for m in range(2)]

                        def dstf(i0, n_, t0=t0, n4=n4):
                            return KT[:, :, t0 * 128:(t0 + n4) * 128].rearrange("p m (t c) -> p t m c", c=128)
                        bk = tp_i[0] % 2
                        tp_i[0] += 1
                        ptv = PST[bk]
                        for j, src in enumerate(blocks):
                            emit("pe", lambda e, j=j, src=src, ptv=ptv: e.transpose(ptv[:, j, :], src, ident[:]),
                                 reads=[kload_b[t0 // 8], *CONST], writes=[PST_b[bk]], signal=(j == len(blocks) - 1))
                        for m in range(2):
                            emit("dve", lambda e, ptv=ptv, t0=t0, n4=n4, m=m: e.tensor_copy(
                                out=KT[:, m, t0 * 128:(t0 + n4) * 128].rearrange("p (t c) -> p t c", c=128),
                                in_=ptv[:, 0:2 * n4, :].rearrange("p (t m) c -> p t m c", m=2)[:, :, m, :]),
                                reads=[PST_b[bk]], writes=[KT_b])
                    for q0 in range(0, NT, 4):
                        nq = min(4, NT - q0)
                        ql, ql_b = qload.get()
                        g_, g_b = gl.get()
                        emit("sp", lambda e, ql=ql, q0=q0, nq=nq, hc=hc: e.dma_start(
                            out=ql[:, 0:nq, :], in_=scr[nm]["q"][q0 * 128:(q0 + nq) * 128, hc].rearrange("(t p) c -> p t c", p=128)),
                            reads=[dbuf(nm, "q", q0 + i) for i in range(nq)], writes=[ql_b], dma=True)
                        emit("sp", lambda e, g_=g_, q0=q0, nq=nq, hc=hc: e.dma_start(
                            out=g_[:, 0:nq, :], in_=scr[nm]["g"][q0 * 128:(q0 + nq) * 128, hc].rearrange("(t p) c -> p t c", p=128)),
                            reads=[dbuf(nm, "g", q0 + i) for i in range(nq)], writes=[g_b], dma=True)
                        qt, qt_b = QT.get()
                        bk = tp_i[0] % 2
                        tp_i[0] += 1
                        ptv = PST[bk]
                        blocks = [ql[:, tt, m * 128:(m + 1) * 128] for tt in range(nq) for m in range(2)]
                        for j, src in enumerate(blocks):
                            emit("pe", lambda e, j=j, src=src, ptv=ptv: e.transpose(ptv[:, j, :], src, ident[:]),
                                 reads=[ql_b, *CONST], writes=[PST_b[bk]], signal=(j == len(blocks) - 1))
                        for m in range(2):
                            emit("act", lambda e, ptv=ptv, nq=nq, m=m, qt=qt: e.activation(
                                out=qt[:, m, 0:nq * 128].rearrange("p (t c) -> p t c", c=128),
                                in_=ptv[:, 0:2 * nq, :].rearrange("p (t m) c -> p t m c", m=2)[:, :, m, :], func=AF.Copy),
                                reads=[PST_b[bk]], writes=[qt_b])
                        ot, ot_b = o0.get()
                        for m in range(2):
                            def emit_qk(kt, m=m, qt=qt, qt_b=qt_b, nq=nq):
                                bs = 4 + (sbank[0] % 2)
                                sbank[0] += 1
                                emit("pe", lambda e, bs=bs, m=m, kt=kt, qt=qt, nq=nq: e.matmul(
                                    PSB[bs][:, 0:nq * 128], KT[:, m, kt * 128:(kt + 1) * 128], qt[:, m, 0:nq * 128],
                                    start=True, stop=True), reads=[KT_b, qt_b], writes=[PSB_b[bs]])
                                pt, pt_b = Pt.get()
                                bias = ptab_s[:, 2:3] if kt == 0 else zero_c[:, 0:1]
                                emit("act", lambda e, pt=pt, bs=bs, nq=nq, bias=bias: e.activation(
                                    out=pt[:, 0:nq * 128], in_=PSB[bs][:, 0:nq * 128], func=AF.Exp, bias=bias, scale=scale),
                                    reads=[PSB_b[bs], *CONST], writes=[pt_b])
                                return pt, pt_b

                            pend = emit_qk(0)
                            for kt in range(NT):
                                nxt = emit_qk(kt + 1) if kt + 1 < NT else None
                                pt, pt_b = pend
                                for qi in range(nq):
                                    emit("pe", lambda e, qi=qi, pt=pt, kt=kt: e.matmul(
                                        PSB[qi][:, 0:257], pt[:, qi * 128:(qi + 1) * 128], Vx[:, kt, :],
                                        start=(kt == 0), stop=(kt == NT - 1)),
                                        reads=[pt_b, Vx_b[kt // 8]], writes=[PSB_b[qi]], signal=(kt == NT - 1))
                                pend = nxt
                            for qi in range(nq):
                                stt, st_b = stat.get()
                                emit("dve", lambda e, stt=stt, qi=qi: e.reciprocal(out=stt[:, 0:1], in_=PSB[qi][:, 256:257]),
                                     reads=[PSB_b[qi]], writes=[st_b])
                                if m == 0:
                                    emit("dve", lambda e, ot=ot, qi=qi, stt=stt: e.tensor_scalar(
                                        out=ot[:, qi, :], in0=PSB[qi][:, 0:256], scalar1=stt[:, 0:1], scalar2=None, op0=ALU.mult),
                                        reads=[PSB_b[qi], st_b], writes=[ot_b])
                                else:
                                    emit("dve", lambda e, stt=stt: e.tensor_tensor(
                                        out=stt[:, 1:2], in0=stt[:, 0:1], in1=lam_t[:, 4:5], op=ALU.mult),
                                        reads=[st_b, *CONST], writes=[st_b])
                                    emit("dve", lambda e, ot=ot, qi=qi, stt=stt: e.scalar_tensor_tensor(
                                        out=ot[:, qi, :], in0=PSB[qi][:, 0:256], scalar=stt[:, 1:2], in1=ot[:, qi, :],
                                        op0=ALU.mult, op1=ALU.add),
                                        reads=[PSB_b[qi], st_b, ot_b], writes=[ot_b])
                        for qi in range(nq):
                            jk, jk_b = junk.get()
                            stt, st_b = rms_rstd(stat, jk[:], jk_b, ot[:, qi, :], ot_b, 256, 1e-5)
                            tm, tm_b = tmpo.get()
                            emit("dve", lambda e, tm=tm, ot=ot, qi=qi, stt=stt: e.scalar_tensor_tensor(
                                out=tm[:], in0=ot[:, qi, :], scalar=stt[:, 3:4], in1=subln_s[:], op0=ALU.mult, op1=ALU.mult),
                                reads=[ot_b, st_b, *CONST], writes=[tm_b])
                            o_, o_b = ob.get()
                            emit("dve", lambda e, o_=o_, tm=tm, g_=g_, qi=qi: e.scalar_tensor_tensor(
                                out=o_[:], in0=tm[:], scalar=float(1.0 - lam_init), in1=g_[:, qi, :], op0=ALU.mult, op1=ALU.mult),
                                reads=[tm_b, g_b], writes=[o_b])
                            t = q0 + qi
                            dst = scr[nm]["o"][t * 128:(t + 1) * 128, hc]
                            emit("pool", lambda e, dst=dst, o_=o_: e.dma_start(out=dst, in_=o_[:]),
                                 reads=[o_b], writes=[dbuf(nm, "o", t)], dma=True)
            return body

        def ph_outproj(nm, layer):
            S = dict(seqs)[nm]
            NT = S // 128 + 1
            wout = rwout if layer == 0 else dwout
            KO = 2 * KC

            def body(st):
                AT2 = tsb(st, "AT2", [128, KO, TB_OUT * 128], BF16)
                AT_b = [Buf() for _ in range(TB_OUT)]
                WT = TPool(st, "WT", [128, KC, 512], BF16, 2)
                orow = TPool(st, "orow", [128, 2 * D], BF16, 2)
                stg = TPool(st, "stg", [128, 512], F32, 3)
                for blk in range(0, NT, TB_OUT):
                    tiles = list(range(blk, min(NT, blk + TB_OUT)))
                    for ti, t in enumerate(tiles):
                        ot, ot_b = orow.get()
                        emit("sp", lambda e, ot=ot, t=t: e.dma_start(out=ot[:], in_=scr[nm]["o"][t * 128:(t + 1) * 128, 0:2 * D]),
                             reads=[dbuf(nm, "o", t)], writes=[ot_b], dma=True)
                        transposes([ot[:, c * 128:(c + 1) * 128] for c in range(KO)], [ot_b],
                                   lambda i0, n, ti=ti: AT2[:, i0:i0 + n, ti * 128:(ti + 1) * 128], [AT_b[ti]])
                    for cb in range(D // 512):
                        c0 = cb * 512
                        for half in range(2):
                            wt, wb = WT.get()
                            nparts = max(1, KC // 8)
                            per = KC // nparts
                            toks = {}
                            for part in range(nparts):
                                k0 = half * KC * 128 + part * per * 128
                                src = wout[k0:k0 + per * 128, c0:c0 + 512].rearrange("(a p) c -> p a c", p=128)
                                tmpb = Buf()
                                if part == 0:
                                    tmpb.w, tmpb.r = wb.w, wb.r
                                tok = emit("pool", lambda e, wt=wt, src=src, part=part, per=per: e.dma_start(
                                    out=wt[:, part * per:(part + 1) * per, :], in_=src), writes=[tmpb], dma=True)
                                toks[tok[0]] = (tok[1], tok[2])
                            wb.w, wb.r = toks, {}
                            for ti, t in enumerate(tiles):
                                for kc in range(KC):
                                    first = (half == 0 and kc == 0)
                                    last = (half == 1 and kc == KC - 1)
                                    emit("pe", lambda e, ti=ti, kc=kc, wt=wt, half=half, first=first, last=last: e.matmul(
                                        PSB[ti][:, :], AT2[:, half * KC + kc, ti * 128:(ti + 1) * 128], wt[:, kc, :],
                                        start=first, stop=last),
                                        reads=[AT_b[ti], wb], writes=[PSB_b[ti]], signal=(kc == KC - 1))
                        for ti, t in enumerate(tiles):
                            sg, sg_b = stg.get()
                            emit("act", lambda e, sg=sg, ti=ti: e.activation(out=sg[:], in_=PSB[ti][:, :], func=AF.Copy),
                                 reads=[PSB_b[ti]], writes=[sg_b])
                            dst = scr[nm]["m"][t * 128:(t + 1) * 128, c0:c0 + 512]
                            emit("pool", lambda e, dst=dst, sg=sg: e.dma_start(out=dst, in_=sg[:]),
                                 reads=[sg_b], writes=[dbuf(nm, "m", t)], dma=True)
            return body

        def ph_norm(nm, layer):
            S = dict(seqs)[nm]
            NT = S // 128 + 1

            def body(st):
                xrow = TPool(st, "xrow", [128, D], F32, 2)
                mrow = TPool(st, "mrow", [128, D], F32, 2)
                jrow = TPool(st, "jrow", [128, D], BF16, 2)
                stat = TPool(st, "stat", [128, 8], F32, 4)
                load_gain(post_n[layer:layer + 1, :])
                for t in range(NT):
                    if layer == 1 and t == 0:
                        continue
                    xt, xb = xrow.get()
                    load_x(nm, layer, t, xt, xb)
                    mt, mb = mrow.get()
                    emit("sp", lambda e, mt=mt, t=t: e.dma_start(out=mt[:], in_=scr[nm]["m"][t * 128:(t + 1) * 128, :]),
                         reads=[dbuf(nm, "m", t)], writes=[mb], dma=True)
                    jt, jb = jrow.get()
                    stt, st_b = rms_rstd(stat, jt[:], jb, mt[:], mb, D, 1e-6)
                    emit("dve", lambda e, mt=mt, stt=stt: e.scalar_tensor_tensor(
                        out=mt[:], in0=mt[:], scalar=stt[:, 3:4], in1=gain_t[:], op0=ALU.mult, op1=ALU.mult),
                        reads=[mb, st_b, gain_b], writes=[mb])
                    emit("dve", lambda e, mt=mt, xt=xt: e.tensor_tensor(out=xt[:], in0=xt[:], in1=mt[:], op=ALU.add),
                         reads=[mb, xb], writes=[xb])
                    if layer == 0:
                        dst = scr[nm]["x1"][t * 128:(t + 1) * 128, :]
                        wbuf = dbuf(nm, "x1", t)
                    else:
                        dst = yout[nm][(t - 1) * 128:t * 128, :]
                        wbuf = dbuf(nm, "y", t)
                    emit("pool", lambda e, dst=dst, xt=xt: e.dma_start(out=dst, in_=xt[:]),
                         reads=[xb], writes=[wbuf], dma=True)
            return body

        run_phase(ph_const)
        stop_after = cfg.get("stop_after")
        for nm, S in seqs:
            plist = [("in0", ph_inproj(nm, 0)), ("ret", ph_ret(nm)), ("out0", ph_outproj(nm, 0)), ("norm0", ph_norm(nm, 0)),
                     ("in1", ph_inproj(nm, 1)), ("att", ph_att(nm)), ("out1", ph_outproj(nm, 1)), ("norm1", ph_norm(nm, 1))]
            for pname, ph in plist:
                run_phase(ph)
                if stop_after == pname:
                    break
    return nc


def _rot_table(L, rot_dim, theta):
    half = rot_dim // 2
    inv_freq = np.power(np.float32(theta), -np.arange(half, dtype=np.float32) * np.float32(2.0) / np.float32(rot_dim)).astype(np.float32)
    pos = np.arange(L, dtype=np.float32)
    ang = (pos[:, None] * inv_freq[None, :]).astype(np.float32)
    return np.cos(ang).astype(np.float32), np.sin(ang).astype(np.float32)


def _const_tables(cfg):
    Lmax = max(cfg["S_S"], cfg["S_P"]) + 128
    tabs = {}
    for nm, rd, th in (("rtab", 256, RET_THETA), ("dtab", 32, ROPE_THETA)):
        c, s = _rot_table(Lmax, rd, th)
        t = np.zeros((Lmax, 2, rd // 2), np.float32)
        t[:, 0, :] = 1.0
        t[PADR:, 0, :] = c[:Lmax - PADR]
        t[PADR:, 1, :] = s[:Lmax - PADR]
        tabs[nm] = t
    j = np.arange(128, dtype=np.float32)[:, None]
    i = np.arange(128, dtype=np.float32)[None, :]
    ctab = np.zeros((128, 6, 128), np.float32)
    ctab[:, 0, :] = np.maximum(i - j, 0)
    ctab[:, 1, :] = (i >= j)
    ctab[:, 2, :] = np.maximum(j - i, 0)
    ctab[:, 3, :] = (j > i)
    ctab[:, 4, :] = i + 1 + 0 * j
    ctab[:, 5, :] = 128 - i + 0 * j
    ptab = np.zeros((128, 4), np.float32)
    ptab[:, 0] = 127 - np.arange(128)
    ptab[:, 1] = np.arange(128)
    ptab[:PADR, 2] = -30000.0
    tabs["ctab"] = ctab
    tabs["ptab"] = ptab
    tabs["ident"] = np.eye(128, dtype=np.float32)
    return tabs


_PROG_CACHE = {}


def run_cfg(cfg, inputs, debug=False):
    key = (tuple(sorted(cfg.items())), debug)
    if key not in _PROG_CACHE:
        _PROG_CACHE[key] = build_program(cfg, debug=debug)
    nc = _PROG_CACHE[key]
    f = lambda a: np.ascontiguousarray(np.asarray(a, dtype=np.float32))
    xp, xs = f(inputs["x_prompt"]), f(inputs["x_sample"])
    shared = dict(
        meta=f(inputs["meta_tokens"]), pre_norm=f(inputs["pre_norm"]), post_norm=f(inputs["post_norm"]),
        ret_w_in=f(inputs["ret_w_in"][0]), ret_w_out=f(inputs["ret_w_out"][0]),
        ret_decay=f(np.concatenate([np.asarray(inputs["ret_decay_fwd"]), np.asarray(inputs["ret_decay_bwd"])], 0)),
        diff_w_in=f(inputs["diff_w_in"][0]), diff_w_out=f(inputs["diff_w_out"][0]),
        diff_lam=f(np.concatenate([np.asarray(inputs[k]) for k in
                                   ("diff_lambda_q1", "diff_lambda_k1", "diff_lambda_q2", "diff_lambda_k2")], 0)),
        diff_subln=f(inputs["diff_subln"]),
    )
    shared.update(_const_tables(cfg))
    in_maps = []
    for c in range(8):
        m = dict(shared)
        m["x_p"] = xp[c]
        m["x_s"] = xs[c % xs.shape[0]]
        in_maps.append(m)
    res = run_bass_kernel_spmd(nc, in_maps, core_ids=list(range(8)))
    y_p = np.stack([res.results[c]["y_p"] for c in range(8)], 0)
    y_s = np.stack([res.results[c]["y_s"] for c in range(xs.shape[0])], 0)
    return (y_p, y_s), res


def kernel(**inputs):
    (y_p, y_s), _ = run_cfg(FULL, inputs)
    return (y_p.astype(np.float32), y_s.astype(np.float32))
```

```python
import math
from contextlib import ExitStack
import numpy as np
import concourse.bass as bass
import concourse.mybir as mybir
from concourse.bass_utils import run_bass_kernel_spmd

F32 = mybir.dt.float32
BF16 = mybir.dt.bfloat16
AF = mybir.ActivationFunctionType
ALU = mybir.AluOpType
AX = mybir.AxisListType

FULL = dict(D=4096, HR=16, HD=32, S_P=2048, S_S=8192)
N_META = 16
PADR = 112
RET_THETA = 10000.0
ROPE_THETA = 500000.0
EPOCH = 28000


class Buf:
    __slots__ = ("w", "r")

    def __init__(self):
        self.w = {}
        self.r = {}


class _Eng:
    def __init__(self, name):
        self.name = name
        self.ops = []
        self.sem = None
        self.epoch = 0
        self.count = 0
        self.pending = False
        self.known = {}
        self.slots = []
        self.slot_i = 0
        self.last = None


class Sched:
    def __init__(self, nc, stack, n_slots):
        self.nc = nc
        self.stack = stack
        self.E = {n: _Eng(n) for n in ("pe", "act", "dve", "pool", "sp")}
        self.nsem = 0
        for n, e in self.E.items():
            e.sem = self._newsem(n)
        for n, k in n_slots.items():
            for i in range(k):
                self.E[n].slots.append([self._newsem(f"{n}d{i}"), 0, (n, "slot", i)])

    def _newsem(self, name):
        self.nsem += 1
        return self.stack.enter_context(self.nc.semaphore(f"s{self.nsem}_{name}"))

    def _need(self, eng, waits, key, sem, val):
        if val <= 0:
            return
        if eng.name == "pe" and key[0] == "pe" and key[1] != "slot":
            return
        if key[0] == eng.name and key[1] != "slot":
            if key[1] != eng.epoch or eng.count - val >= 2:
                return
        if eng.known.get(key, 0) >= val:
            return
        cur = waits.get(key)
        if cur is None or cur[1] < val:
            waits[key] = (sem, val)

    def emit(self, en, fn, reads=(), writes=(), dma=False, signal=True):
        eng = self.E[en]
        waits = {}
        for b in reads:
            for k, (s, v) in b.w.items():
                self._need(eng, waits, k, s, v)
        for b in writes:
            for k, (s, v) in b.w.items():
                self._need(eng, waits, k, s, v)
            for k, (s, v) in b.r.items():
                self._need(eng, waits, k, s, v)
        if dma:
            slot = eng.slots[eng.slot_i]
            eng.slot_i = (eng.slot_i + 1) % len(eng.slots)
            self._need(eng, waits, slot[2], slot[0], slot[1])
            slot[1] += 16
            tok = (slot[2], slot[0], slot[1])
            inc = (slot[0], 16)
        else:
            if eng.count >= EPOCH and not eng.pending:
                eng.epoch += 1
                eng.count = 0
                eng.sem = self._newsem(f"{en}e{eng.epoch}")
            key = (en, eng.epoch)
            if signal:
                eng.count += 1
                eng.pending = False
                tok = (key, eng.sem, eng.count)
                inc = (eng.sem, 1)
            else:
                eng.pending = True
                tok = (key, eng.sem, eng.count + 1)
                inc = None
            eng.last = tok
        for k, (s, v) in waits.items():
            eng.known[k] = v
        eng.ops.append((list(waits.values()), fn, inc))
        k, s, v = tok
        for b in reads:
            cur = b.r.get(k)
            if cur is None or cur[1] < v:
                b.r[k] = (s, v)
        for b in writes:
            b.w = {k: (s, v)}
            b.r = {}
        return tok

    def barrier(self):
        toks = []
        for e in self.E.values():
            assert not e.pending
            if e.last is not None:
                toks.append(e.last)
            for slot in e.slots:
                if slot[1] > 0:
                    toks.append((slot[2], slot[0], slot[1]))
        for e in self.E.values():
            waits = {}
            for k, s, v in toks:
                if k[0] == e.name and k[1] != "slot":
                    continue
                if e.known.get(k, 0) >= v:
                    continue
                waits[k] = (s, v)
                e.known[k] = v
            if waits:
                e.ops.append((list(waits.values()), None, None))

    def flush(self, block):
        def mk(en):
            def run(engobj):
                for waits, fn, inc in self.E[en].ops:
                    for s, v in waits:
                        engobj.wait_ge(s, v)
                    if fn is None:
                        continue
                    ins = fn(engobj)
                    if inc is not None:
                        ins.then_inc(inc[0], inc[1])
                self.E[en].ops = []
            return run
        block.tensor(mk("pe"))
        block.scalar(mk("act"))
        block.vector(mk("dve"))
        block.gpsimd(mk("pool"))
        block.sync(mk("sp"))


def build_program(cfg, debug=False):
    D = cfg["D"]
    HR = cfg["HR"]
    HD = cfg["HD"]
    KC = D // 128
    RQK = HR * 256
    RV = HR * 512
    DQK = HD * 256
    DV = HD * 256
    assert RV == 2 * D and DV == 2 * D
    NR_IN = 2 * RQK + 2 * RV
    ND_IN = 2 * DQK + 2 * DV
    lam_init = 0.8 - 0.6 * math.exp(-0.3 * 1)
    seqs = [("s", cfg["S_S"]), ("p", cfg["S_P"])]
    TB_IN = cfg.get("TB_IN", 8)
    TB_OUT = cfg.get("TB_OUT", 4)

    nc = bass.Bass("TRN2", target_bir_lowering=False)
    okind = "ExternalOutput" if debug else "Internal"

    def dram(name, shape, dt, kind):
        return nc.dram_tensor(name, list(shape), dt, kind=kind).ap()

    xin, yout = {}, {}
    for nm, S in seqs:
        xin[nm] = dram(f"x_{nm}", [S, D], F32, "ExternalInput")
        yout[nm] = dram(f"y_{nm}", [S, D], F32, "ExternalOutput")
    meta = dram("meta", [N_META, D], F32, "ExternalInput")
    pre_n = dram("pre_norm", [2, D], F32, "ExternalInput")
    post_n = dram("post_norm", [2, D], F32, "ExternalInput")
    rwin = dram("ret_w_in", [D, NR_IN], F32, "ExternalInput")
    rwout = dram("ret_w_out", [RV, D], F32, "ExternalInput")
    rdec = dram("ret_decay", [2, HR], F32, "ExternalInput")
    dwin = dram("diff_w_in", [D, ND_IN], F32, "ExternalInput")
    dwout = dram("diff_w_out", [DV, D], F32, "ExternalInput")
    dlam = dram("diff_lam", [4, 128], F32, "ExternalInput")
    dsub = dram("diff_subln", [1, 256], F32, "ExternalInput")
    Lmax = max(S for _, S in seqs) + 128
    rtab = dram("rtab", [Lmax, 2, 128], F32, "ExternalInput")
    dtab = dram("dtab", [Lmax, 2, 16], F32, "ExternalInput")
    ctab = dram("ctab", [128, 6, 128], F32, "ExternalInput")
    ptab = dram("ptab", [128, 4], F32, "ExternalInput")
    ident_in = dram("ident", [128, 128], F32, "ExternalInput")

    scr = {}
    dbufs = {}
    for nm, S in seqs:
        Lp = S + 128
        scr[nm] = dict(
            q=dram(f"q_{nm}", [Lp, 2 * D], BF16, okind),
            k=dram(f"k_{nm}", [Lp, 2 * D], BF16, okind),
            v=dram(f"v_{nm}", [Lp, 2 * D], BF16, okind),
            g=dram(f"g_{nm}", [Lp, 2 * D], BF16, okind),
            o=dram(f"o_{nm}", [Lp, 2 * D], BF16, okind),
            m=dram(f"m_{nm}", [Lp, D], F32, okind),
            x1=dram(f"x1_{nm}", [Lp, D], F32, okind),
        )

    def dbuf(*key):
        b = dbufs.get(key)
        if b is None:
            b = dbufs[key] = Buf()
        return b

    top = ExitStack()
    with top:
        sch = Sched(nc, top, {"sp": 16, "pool": 12})
        emit = sch.emit

        uid = [0]

        def tsb(stack, name, shape, dt):
            uid[0] += 1
            return stack.enter_context(nc.sbuf_tensor(f"sb{uid[0]}_{name}", list(shape), dt))

        ident = tsb(top, "ident", [128, 128], BF16)
        ctab_s = tsb(top, "ctab_s", [128, 6, 128], F32)
        ptab_s = tsb(top, "ptab_s", [128, 4], F32)
        subln_s = tsb(top, "subln_s", [128, 256], F32)
        lg = tsb(top, "lg", [128, 2, HR], F32)
        cdec = tsb(top, "cdec", [128, 2, HR], F32)
        lam_t = tsb(top, "lam_t", [128, 8], F32)
        zero_c = tsb(top, "zero_c", [128, 1], F32)
        gain_t = tsb(top, "gain_t", [128, D], F32)
        gain_b = Buf()
        CB = Buf()
        CONST = [CB]

        PSB = [top.enter_context(nc.psum_tensor(f"psb{i}", [128, 512], F32)) for i in range(6)]
        PST = [top.enter_context(nc.psum_tensor(f"pst{i}", [128, 8, 128], BF16)) for i in range(2)]
        PSB_b = [Buf() for _ in range(6)]
        PST_b = [Buf() for _ in range(2)]
        bank_i = [0]
        tp_i = [0]

        def nextbank():
            i = bank_i[0]
            bank_i[0] = (i + 1) % 6
            return i

        class TPool:
            def __init__(self, stack, name, shape, dt, n):
                self.t = [tsb(stack, f"{name}{i}", shape, dt) for i in range(n)]
                self.b = [Buf() for _ in range(n)]
                self.i = 0

            def get(self):
                i = self.i
                self.i = (i + 1) % len(self.t)
                return self.t[i], self.b[i]

        def run_phase(fn):
            with ExitStack() as st, nc.Block() as block:
                fn(st)
                sch.barrier()
                sch.flush(block)

        def ph_const(st):
            ident_f = tsb(st, "ident_f", [128, 128], F32)
            dec_raw = tsb(st, "dec_raw", [128, 2, HR], F32)
            lam_in = tsb(st, "lam_in", [128, 4, 128], F32)
            B0 = Buf()
            emit("sp", lambda e: e.dma_start(out=ident_f[:], in_=ident_in[:, :]), writes=[B0], dma=True)
            emit("sp", lambda e: e.dma_start(out=ctab_s[:], in_=ctab[:, :, :]), writes=[CB], dma=True)
            emit("sp", lambda e: e.dma_start(out=ptab_s[:], in_=ptab[:, :]), writes=[CB], dma=True)
            emit("sp", lambda e: e.dma_start(out=subln_s[:], in_=dsub[0:1, :].partition_broadcast(128)), writes=[CB], dma=True)
            for i in range(2):
                emit("sp", lambda e, i=i: e.dma_start(out=dec_raw[:, i, :], in_=rdec[i:i + 1, :].partition_broadcast(128)),
                     writes=[B0], dma=True)
            for i in range(4):
                emit("sp", lambda e, i=i: e.dma_start(out=lam_in[:, i, :], in_=dlam[i:i + 1, :].partition_broadcast(128)),
                     writes=[B0], dma=True)
            emit("dve", lambda e: e.tensor_copy(out=ident[:], in_=ident_f[:]), reads=[B0], writes=[CB])
            emit("dve", lambda e: e.memset(zero_c[:], 0.0), writes=[CB])
            emit("act", lambda e: e.activation(out=lg[:], in_=dec_raw[:], func=AF.Exp), reads=[B0], writes=[CB])
            emit("dve", lambda e: e.tensor_scalar(out=lg[:], in0=lg[:], scalar1=-1.0, scalar2=None, op0=ALU.mult),
                 reads=[CB], writes=[CB])
            emit("act", lambda e: e.activation(out=cdec[:], in_=lg[:], func=AF.Exp, scale=128.0), reads=[CB], writes=[CB])
            emit("dve", lambda e: e.tensor_tensor(out=lam_in[:, 0, :], in0=lam_in[:, 0, :], in1=lam_in[:, 1, :], op=ALU.mult),
                 reads=[B0], writes=[B0])
            emit("dve", lambda e: e.tensor_tensor(out=lam_in[:, 2, :], in0=lam_in[:, 2, :], in1=lam_in[:, 3, :], op=ALU.mult),
                 reads=[B0], writes=[B0])
            emit("dve", lambda e: e.reduce_sum(out=lam_t[:, 0:1], in_=lam_in[:, 0, :], axis=AX.X), reads=[B0], writes=[CB])
            emit("dve", lambda e: e.reduce_sum(out=lam_t[:, 1:2], in_=lam_in[:, 2, :], axis=AX.X), reads=[B0], writes=[CB])
            emit("act", lambda e: e.activation(out=lam_t[:, 2:4], in_=lam_t[:, 0:2], func=AF.Exp), reads=[CB], writes=[CB])
            emit("dve", lambda e: e.tensor_tensor(out=lam_t[:, 4:5], in0=lam_t[:, 3:4], in1=lam_t[:, 2:3], op=ALU.subtract),
                 reads=[CB], writes=[CB])
            emit("dve", lambda e: e.tensor_scalar(out=lam_t[:, 4:5], in0=lam_t[:, 4:5], scalar1=-lam_init, scalar2=None,
                                                  op0=ALU.add), reads=[CB], writes=[CB])

        def load_gain(row_ap):
            emit("sp", lambda e: e.dma_start(out=gain_t[:], in_=row_ap.partition_broadcast(128)), writes=[gain_b], dma=True)

        def rms_rstd(stat, junk_ap, junk_b, src_ap, src_b, ncols, eps):
            stt, st_b = stat.get()
            emit("act", lambda e: e.activation(out=junk_ap, in_=src_ap, func=AF.Square, accum_out=stt[:, 0:1]),
                 reads=[src_b], writes=[junk_b, st_b])
            emit("dve", lambda e: e.tensor_scalar(out=stt[:, 1:2], in0=stt[:, 0:1], scalar1=1.0 / ncols, scalar2=eps,
                                                  op0=ALU.mult, op1=ALU.add), reads=[st_b], writes=[st_b])
            emit("act", lambda e: e.activation(out=stt[:, 2:3], in_=stt[:, 1:2], func=AF.Sqrt), reads=[st_b], writes=[st_b])
            emit("dve", lambda e: e.reciprocal(out=stt[:, 3:4], in_=stt[:, 2:3]), reads=[st_b], writes=[st_b])
            return stt, st_b

        def transposes(src_blocks, src_b, dst_fn, dst_bufs, evac="dve"):
            i0 = 0
            nblk = len(src_blocks)
            while i0 < nblk:
                n = min(8, nblk - i0)
                bk = tp_i[0] % 2
                tp_i[0] += 1
                ptv = PST[bk]
                for j in range(n):
                    src = src_blocks[i0 + j]
                    emit("pe", lambda e, j=j, src=src, ptv=ptv: e.transpose(ptv[:, j, :], src, ident[:]),
                         reads=[*src_b, *CONST], writes=[PST_b[bk]], signal=(j == n - 1))
                dst = dst_fn(i0, n)
                if evac == "act":
                    emit(evac, lambda e, dst=dst, ptv=ptv, n=n: e.copy(out=dst, in_=ptv[:, 0:n, :]),
                         reads=[PST_b[bk]], writes=dst_bufs)
                else:
                    emit(evac, lambda e, dst=dst, ptv=ptv, n=n: e.tensor_copy(out=dst, in_=ptv[:, 0:n, :]),
                         reads=[PST_b[bk]], writes=dst_bufs)
                i0 += n

        def x_rows(nm, layer, t):
            if layer == 0:
                return xin[nm][(t - 1) * 128:t * 128, :] if t > 0 else None
            return scr[nm]["x1"][t * 128:(t + 1) * 128, :]

        def load_x(nm, layer, t, tl, tb):
            if layer == 0 and t == 0:
                emit("dve", lambda e: e.memset(tl[:], 0.0), writes=[tb])
                emit("sp", lambda e: e.dma_start(out=tl[PADR:128, :], in_=meta[:, :]), writes=[tb], dma=True)
            else:
                src = x_rows(nm, layer, t)
                rb = [dbuf(nm, "x1", t)] if layer == 1 else []
                emit("sp", lambda e: e.dma_start(out=tl[:], in_=src), reads=rb, writes=[tb], dma=True)

        def ph_inproj(nm, layer):
            S = dict(seqs)[nm]
            NT = S // 128 + 1
            win = rwin if layer == 0 else dwin
            QK = RQK if layer == 0 else DQK
            VV = RV if layer == 0 else DV
            NIN = 2 * QK + 2 * VV

            def body(st):
                AT = tsb(st, "AT", [128, KC, TB_IN * 128], BF16)
                AT_b = [Buf() for _ in range(TB_IN)]
                WT = TPool(st, "WT", [128, KC, 512], BF16, 2)
                xrow = TPool(st, "xrow", [128, D], F32, 1)
                hrow = TPool(st, "hrow", [128, D], BF16, 2)
                stat = TPool(st, "stat", [128, 8], F32, 4)
                stg = TPool(st, "stg", [128, 512], F32, 3)
                obf = TPool(st, "obf", [128, 512], BF16, 4)
                rot = TPool(st, "rot", [128, 4, 256], F32, 2)
                tw = 128 if layer == 0 else 16
                tabs = tsb(st, "tabs", [128, TB_IN, 2, tw], F32)
                tabs_b = Buf()
                tabd = rtab if layer == 0 else dtab
                load_gain(pre_n[layer:layer + 1, :])
                for blk in range(0, NT, TB_IN):
                    tiles = list(range(blk, min(NT, blk + TB_IN)))
                    nt = len(tiles)
                    emit("sp", lambda e, blk=blk, nt=nt: e.dma_start(
                        out=tabs[:, 0:nt, :, :],
                        in_=tabd[blk * 128:(blk + nt) * 128, :, :].rearrange("(t p) a c -> p t a c", p=128)),
                        writes=[tabs_b], dma=True)
                    for ti, t in enumerate(tiles):
                        xt, xb = xrow.get()
                        load_x(nm, layer, t, xt, xb)
                        ht, hb = hrow.get()
                        stt, st_b = rms_rstd(stat, ht[:], hb, xt[:], xb, D, 1e-6)
                        emit("dve", lambda e, ht=ht, xt=xt, stt=stt: e.scalar_tensor_tensor(
                            out=ht[:], in0=xt[:], scalar=stt[:, 3:4], in1=gain_t[:], op0=ALU.mult, op1=ALU.mult),
                            reads=[xb, st_b, gain_b], writes=[hb])
                        blocks = [ht[:, c * 128:(c + 1) * 128] for c in range(KC)]
                        transposes(blocks, [hb],
                                   lambda i0, n, ti=ti: AT[:, i0:i0 + n, ti * 128:(ti + 1) * 128], [AT_b[ti]])
                    for cb in range(NIN // 512):
                        c0 = cb * 512
                        if c0 < QK:
                            sec, sc0 = "q", c0
                        elif c0 < 2 * QK:
                            sec, sc0 = "k", c0 - QK
                        elif c0 < 2 * QK + VV:
                            sec, sc0 = "v", c0 - 2 * QK
                        else:
                            sec, sc0 = "g", c0 - 2 * QK - VV
                        wt, wb = WT.get()
                        nparts = max(1, KC // 8)
                        per = KC // nparts
                        toks = {}
                        for part in range(nparts):
                            src = win[part * per * 128:(part + 1) * per * 128, c0:c0 + 512].rearrange(
                                "(a p) c -> p a c", p=128)
                            tmpb = Buf()
                            if part == 0:
                                tmpb.w, tmpb.r = wb.w, wb.r
                            tok = emit("pool", lambda e, wt=wt, src=src, part=part, per=per: e.dma_start(
                                out=wt[:, part * per:(part + 1) * per, :], in_=src), writes=[tmpb], dma=True)
                            toks[tok[0]] = (tok[1], tok[2])
                        wb.w, wb.r = toks, {}
                        for ti, t in enumerate(tiles):
                            bk = nextbank()
                            for kc in range(KC):
                                emit("pe", lambda e, bk=bk, kc=kc, ti=ti, wt=wt: e.matmul(
                                    PSB[bk][:, :], AT[:, kc, ti * 128:(ti + 1) * 128], wt[:, kc, :],
                                    start=(kc == 0), stop=(kc == KC - 1)),
                                    reads=[AT_b[ti], wb], writes=[PSB_b[bk]], signal=(kc == KC - 1))
                            ob, ob_b = obf.get()
                            if sec == "v":
                                emit("dve", lambda e, ob=ob, bk=bk: e.tensor_copy(out=ob[:], in_=PSB[bk][:, :]),
                                     reads=[PSB_b[bk]], writes=[ob_b])
                            elif sec == "g":
                                emit("act", lambda e, ob=ob, bk=bk: e.activation(out=ob[:], in_=PSB[bk][:, :], func=AF.Silu),
                                     reads=[PSB_b[bk]], writes=[ob_b])
                            else:
                                sg, sg_b = stg.get()
                                scl = (1.0 / 16.0) if (layer == 0 and sec == "q") else 1.0
                                emit("act", lambda e, sg=sg, bk=bk, scl=scl: e.activation(
                                    out=sg[:], in_=PSB[bk][:, :], func=AF.Copy, scale=scl),
                                    reads=[PSB_b[bk]], writes=[sg_b])
                                if layer == 1:
                                    emit("dve", lambda e, ob=ob, sg=sg: e.tensor_copy(out=ob[:], in_=sg[:]),
                                         reads=[sg_b], writes=[ob_b])
                                ngrp = 2 if layer == 0 else 4
                                gw = 512 // ngrp
                                hw = 128 if layer == 0 else 16
                                r, r_b = rot.get()
                                sgv = sg[:].rearrange("p (g w) -> p g w", w=gw)
                                obv = ob[:].rearrange("p (g w) -> p g w", w=gw)
                                x1 = sgv[:, :, 0:hw]
                                x2 = sgv[:, :, hw:2 * hw]
                                o1 = obv[:, :, 0:hw]
                                o2 = obv[:, :, hw:2 * hw]
                                cs = tabs[:, ti, 0:1, :].broadcast_to([128, ngrp, hw])
                                sn = tabs[:, ti, 1:2, :].broadcast_to([128, ngrp, hw])
                                T = [r[:, i, 0:ngrp * hw].rearrange("p (g w) -> p g w", w=hw) for i in range(4)]
                                rd = [sg_b, tabs_b, r_b]
                                emit("dve", lambda e, T=T, x1=x1, cs=cs: e.tensor_tensor(out=T[0], in0=x1, in1=cs, op=ALU.mult),
                                     reads=rd, writes=[r_b])
                                emit("dve", lambda e, T=T, x2=x2, sn=sn: e.tensor_tensor(out=T[1], in0=x2, in1=sn, op=ALU.mult),
                                     reads=rd, writes=[r_b])
                                emit("dve", lambda e, T=T, x2=x2, cs=cs: e.tensor_tensor(out=T[2], in0=x2, in1=cs, op=ALU.mult),
                                     reads=rd, writes=[r_b])
                                emit("dve", lambda e, T=T, x1=x1, sn=sn: e.tensor_tensor(out=T[3], in0=x1, in1=sn, op=ALU.mult),
                                     reads=rd, writes=[r_b])
                                emit("dve", lambda e, T=T, o1=o1: e.tensor_tensor(out=o1, in0=T[0], in1=T[1], op=ALU.subtract),
                                     reads=[r_b], writes=[ob_b])
                                emit("dve", lambda e, T=T, o2=o2: e.tensor_tensor(out=o2, in0=T[2], in1=T[3], op=ALU.add),
                                     reads=[r_b], writes=[ob_b])
                            dst = scr[nm][sec][t * 128:(t + 1) * 128, sc0:sc0 + 512]
                            emit("pool", lambda e, dst=dst, ob=ob: e.dma_start(out=dst, in_=ob[:]),
                                 reads=[ob_b], writes=[dbuf(nm, sec, t)], dma=True)
            return body

        def ph_ret(nm):
            S = dict(seqs)[nm]
            NT = S // 128 + 1

            def body(st):
                oacc = tsb(st, "oacc", [128, NT, 512], F32)
                oacc_b = [Buf() for _ in range(NT)]
                Sst = [tsb(st, f"Sst{d}", [128, 2, 512], F32) for d in range(2)]
                Sbf = [tsb(st, f"Sbf{d}", [128, 2, 512], BF16) for d in range(2)]
                Sst_b = [Buf(), Buf()]
                Sbf_b = [Buf(), Buf()]
                htab = tsb(st, "htab", [128, 5, 128], F32)
                kd = tsb(st, "kd", [128, 2], F32)
                htab_b = Buf()
                qkv = TPool(st, "qkv", [128, 1024], BF16, 3)
                qT = TPool(st, "qT", [128, 4, 128], BF16, 3)
                qTd = TPool(st, "qTd", [128, 2, 128], BF16, 3)
                kdt = TPool(st, "kdt", [128, 256], BF16, 3)
                SD = TPool(st, "SD", [128, 128], BF16, 3)
                gl = TPool(st, "gl", [128, 512], BF16, 2)
                ob = TPool(st, "ob", [128, 512], BF16, 2)
                junk = TPool(st, "junk", [128, 512], F32, 2)
                stat = TPool(st, "stat", [128, 8], F32, 4)
                for h in range(HR):
                    lf = lg[:, 0, h:h + 1]
                    lb = lg[:, 1, h:h + 1]
                    emit("act", lambda e, lf=lf: e.activation(out=htab[:, 0, :], in_=ctab_s[:, 0, :], func=AF.Exp, scale=lf),
                         reads=CONST, writes=[htab_b])
                    emit("act", lambda e, lb=lb: e.activation(out=htab[:, 1, :], in_=ctab_s[:, 2, :], func=AF.Exp, scale=lb),
                         reads=CONST, writes=[htab_b])
                    emit("act", lambda e, lf=lf: e.activation(out=htab[:, 3, :], in_=ctab_s[:, 4, :], func=AF.Exp, scale=lf),
                         reads=CONST, writes=[htab_b])
                    emit("act", lambda e, lb=lb: e.activation(out=htab[:, 4, :], in_=ctab_s[:, 5, :], func=AF.Exp, scale=lb),
                         reads=CONST, writes=[htab_b])
                    emit("act", lambda e, lf=lf: e.activation(out=kd[:, 0:1], in_=ptab_s[:, 0:1], func=AF.Exp, scale=lf),
                         reads=CONST, writes=[htab_b])
                    emit("act", lambda e, lb=lb: e.activation(out=kd[:, 1:2], in_=ptab_s[:, 1:2], func=AF.Exp, scale=lb),
                         reads=CONST, writes=[htab_b])
                    emit("dve", lambda e: e.tensor_tensor(out=htab[:, 0, :], in0=htab[:, 0, :], in1=ctab_s[:, 1, :], op=ALU.mult),
                         reads=[htab_b, *CONST], writes=[htab_b])
                    emit("dve", lambda e: e.tensor_tensor(out=htab[:, 1, :], in0=htab[:, 1, :], in1=ctab_s[:, 3, :], op=ALU.mult),
                         reads=[htab_b, *CONST], writes=[htab_b])
                    emit("dve", lambda e: e.tensor_tensor(out=htab[:, 2, :], in0=htab[:, 0, :], in1=htab[:, 1, :], op=ALU.add),
                         reads=[htab_b], writes=[htab_b])
                    for d in range(2):
                        emit("dve", lambda e, d=d: e.memset(Sst[d][:], 0.0), writes=[Sst_b[d]])
                        emit("dve", lambda e, d=d: e.memset(Sbf[d][:], 0.0), writes=[Sbf_b[d]])
                    for d in range(2):
                        order = range(NT) if d == 0 else range(NT - 1, -1, -1)
                        for n in order:
                            tl, tl_b = qkv.get()
                            r0 = n * 128
                            emit("sp", lambda e, tl=tl, r0=r0, h=h: e.dma_start(
                                out=tl[:, 0:256], in_=scr[nm]["q"][r0:r0 + 128, h * 256:(h + 1) * 256]),
                                reads=[dbuf(nm, "q", n)], writes=[tl_b], dma=True)
                            emit("sp", lambda e, tl=tl, r0=r0, h=h: e.dma_start(
                                out=tl[:, 256:512], in_=scr[nm]["k"][r0:r0 + 128, h * 256:(h + 1) * 256]),
                                reads=[dbuf(nm, "k", n)], writes=[tl_b], dma=True)
                            emit("sp", lambda e, tl=tl, r0=r0, h=h: e.dma_start(
                                out=tl[:, 512:1024], in_=scr[nm]["v"][r0:r0 + 128, h * 512:(h + 1) * 512]),
                                reads=[dbuf(nm, "v", n)], writes=[tl_b], dma=True)
                            qt, qt_b = qT.get()
                            nb = 4 if d == 0 else 2
                            transposes([tl[:, c * 128:(c + 1) * 128] for c in range(nb)], [tl_b],
                                       lambda i0, n_, qt=qt: qt[:, i0:i0 + n_, :], [qt_b], evac="act")
                            qd, qd_b = qTd.get()
                            Rt = htab[:, 3 + d, :]
                            for c in range(2):
                                emit("dve", lambda e, qd=qd, qt=qt, c=c, Rt=Rt: e.tensor_tensor(
                                    out=qd[:, c, :], in0=qt[:, c, :], in1=Rt, op=ALU.mult),
                                    reads=[qt_b, htab_b], writes=[qd_b])
                            kt_, kt_b = kdt.get()
                            emit("dve", lambda e, kt_=kt_, tl=tl, d=d: e.tensor_scalar(
                                out=kt_[:], in0=tl[:, 256:512], scalar1=kd[:, d:d + 1], scalar2=None, op0=ALU.mult),
                                reads=[tl_b, htab_b], writes=[kt_b])
                            bo = nextbank()
                            if d == 0:
                                bs = nextbank()
                                for c in range(2):
                                    emit("pe", lambda e, bs=bs, qt=qt, c=c: e.matmul(
                                        PSB[bs][:, 0:128], qt[:, 2 + c, :], qt[:, c, :], start=(c == 0), stop=(c == 1)),
                                        reads=[qt_b], writes=[PSB_b[bs]], signal=(c == 1))
                                sd, sd_b = SD.get()
                                emit("dve", lambda e, sd=sd, bs=bs: e.tensor_tensor(
                                    out=sd[:], in0=PSB[bs][:, 0:128], in1=htab[:, 2, :], op=ALU.mult),
                                    reads=[PSB_b[bs], htab_b], writes=[sd_b])
                                emit("pe", lambda e, bo=bo, sd=sd, tl=tl: e.matmul(
                                    PSB[bo][:, :], sd[:], tl[:, 512:1024], start=True, stop=False),
                                    reads=[sd_b, tl_b], writes=[PSB_b[bo]], signal=False)
                            for c in range(2):
                                emit("pe", lambda e, bo=bo, qd=qd, c=c, d=d: e.matmul(
                                    PSB[bo][:, :], qd[:, c, :], Sbf[d][:, c, :], start=(d == 1 and c == 0), stop=(c == 1)),
                                    reads=[qd_b, Sbf_b[d]], writes=[PSB_b[bo]], signal=(c == 1))
                            if d == 0:
                                emit("act", lambda e, n=n, bo=bo: e.activation(out=oacc[:, n, :], in_=PSB[bo][:, :], func=AF.Copy),
                                     reads=[PSB_b[bo]], writes=[oacc_b[n]])
                            else:
                                emit("dve", lambda e, n=n, bo=bo: e.tensor_tensor(
                                    out=oacc[:, n, :], in0=oacc[:, n, :], in1=PSB[bo][:, :], op=ALU.add),
                                    reads=[PSB_b[bo], oacc_b[n]], writes=[oacc_b[n]])
                            for c in range(2):
                                bu = nextbank()
                                emit("pe", lambda e, bu=bu, kt_=kt_, tl=tl, c=c: e.matmul(
                                    PSB[bu][:, :], kt_[:, c * 128:(c + 1) * 128], tl[:, 512:1024], start=True, stop=True),
                                    reads=[kt_b, tl_b], writes=[PSB_b[bu]])
                                emit("dve", lambda e, bu=bu, c=c, d=d, h=h: e.scalar_tensor_tensor(
                                    out=Sst[d][:, c, :], in0=Sst[d][:, c, :], scalar=cdec[:, d, h:h + 1], in1=PSB[bu][:, :],
                                    op0=ALU.mult, op1=ALU.add),
                                    reads=[PSB_b[bu], Sst_b[d], *CONST], writes=[Sst_b[d]])
                            emit("act", lambda e, d=d: e.activation(out=Sbf[d][:], in_=Sst[d][:], func=AF.Copy),
                                 reads=[Sst_b[d]], writes=[Sbf_b[d]])
                    for n in range(NT):
                        g_, g_b = gl.get()
                        r0 = n * 128
                        emit("sp", lambda e, g_=g_, r0=r0, h=h: e.dma_start(
                            out=g_[:], in_=scr[nm]["g"][r0:r0 + 128, h * 512:(h + 1) * 512]),
                            reads=[dbuf(nm, "g", n)], writes=[g_b], dma=True)
                        jk, jk_b = junk.get()
                        stt, st_b = rms_rstd(stat, jk[:], jk_b, oacc[:, n, :], oacc_b[n], 512, 1e-6)
                        o_, o_b = ob.get()
                        emit("dve", lambda e, o_=o_, n=n, stt=stt, g_=g_: e.scalar_tensor_tensor(
                            out=o_[:], in0=oacc[:, n, :], scalar=stt[:, 3:4], in1=g_[:], op0=ALU.mult, op1=ALU.mult),
                            reads=[oacc_b[n], st_b, g_b], writes=[o_b])
                        dst = scr[nm]["o"][r0:r0 + 128, h * 512:(h + 1) * 512]
                        emit("pool", lambda e, dst=dst, o_=o_: e.dma_start(out=dst, in_=o_[:]),
                             reads=[o_b], writes=[dbuf(nm, "o", n)], dma=True)
            return body

        def ph_att(nm):
            S = dict(seqs)[nm]
            NT = S // 128 + 1
            Lp = NT * 128
            scale = 128.0 ** -0.5

            def body(st):
                kload = tsb(st, "kload", [128, NT, 256], BF16)
                NCH = (NT + 7) // 8
                kload_b = [Buf() for _ in range(NCH)]
                KT = tsb(st, "KT", [128, 2, Lp], BF16)
                KT_b = Buf()
                Vx = tsb(st, "Vx", [128, NT, 257], BF16)
                Vx_b = [Buf() for _ in range(NCH)]
                qload = TPool(st, "qload", [128, 4, 256], BF16, 2)
                QT = TPool(st, "QT", [128, 2, 512], BF16, 2)
                gl = TPool(st, "gl", [128, 4, 256], BF16, 2)
                Pt = TPool(st, "Pt", [128, 512], BF16, 4)
                o0 = TPool(st, "o0", [128, 4, 256], F32, 2)
                tmpo = TPool(st, "tmpo", [128, 256], F32, 2)
                ob = TPool(st, "ob", [128, 256], BF16, 3)
                junk = TPool(st, "junk", [128, 256], F32, 2)
                stat = TPool(st, "stat", [128, 8], F32, 6)
                emit("dve", lambda e: e.memset(Vx[:, :, 256:257], 1.0), writes=Vx_b)
                sbank = [0]
                for h in range(HD):
                    hc = slice(h * 256, (h + 1) * 256)
                    rdk = [dbuf(nm, "k", t) for t in range(NT)]
                    rdv = [dbuf(nm, "v", t) for t in range(NT)]
                    for ch in range(NCH):
                        ta, tb_ = ch * 8, min(NT, ch * 8 + 8)
                        emit("sp", lambda e, hc=hc, ta=ta, tb_=tb_: e.dma_start(
                            out=kload[:, ta:tb_, :], in_=scr[nm]["k"][ta * 128:tb_ * 128, hc].rearrange("(t p) c -> p t c", p=128)),
                            reads=rdk[ta:tb_], writes=[kload_b[ch]], dma=True)
                        emit("sp", lambda e, hc=hc, ta=ta, tb_=tb_: e.dma_start(
                            out=Vx[:, ta:tb_, 0:256], in_=scr[nm]["v"][ta * 128:tb_ * 128, hc].rearrange("(t p) c -> p t c", p=128)),
                            reads=rdv[ta:tb_], writes=[Vx_b[ch]], dma=True)
                    for t0 in range(0, NT, 4):
                        n4 = min(4, NT - t0)
                        blocks = [kload[:, t0 + tt, m * 128:(m + 1) * 128] for tt in range(n4) for m in range(2)]

                        def dstf(i0, n_, t0=t0, n4=n4):
                            return KT[:, :, t0 * 128:(t0 + n4) * 128].rearrange("p m (t c) -> p t m c", c=128)
                        bk = tp_i[0] % 2
                        tp_i[0] += 1
                        ptv = PST[bk]
                        for j, src in enumerate(blocks):
                            emit("pe", lambda e, j=j, src=src, ptv=ptv: e.transpose(ptv[:, j, :], src, ident[:]),
                                 reads=[kload_b[t0 // 8], *CONST], writes=[PST_b[bk]], signal=(j == len(blocks) - 1))
                        for m in range(2):
                            emit("dve", lambda e, ptv=ptv, t0=t0, n4=n4, m=m: e.tensor_copy(
                                out=KT[:, m, t0 * 128:(t0 + n4) * 128].rearrange("p (t c) -> p t c", c=128),
                                in_=ptv[:, 0:2 * n4, :].rearrange("p (t m) c -> p t m c", m=2)[:, :, m, :]),
                                reads=[PST_b[bk]], writes=[KT_b])
                    for q0 in range(1, NT, 4):
                        nq = min(4, NT - q0)
                        ql, ql_b = qload.get()
                        g_, g_b = gl.get()
                        emit("sp", lambda e, ql=ql, q0=q0, nq=nq, hc=hc: e.dma_start(
                            out=ql[:, 0:nq, :], in_=scr[nm]["q"][q0 * 128:(q0 + nq) * 128, hc].rearrange("(t p) c -> p t c", p=128)),
                            reads=[dbuf(nm, "q", q0 + i) for i in range(nq)], writes=[ql_b], dma=True)
                        emit("sp", lambda e, g_=g_, q0=q0, nq=nq, hc=hc: e.dma_start(
                            out=g_[:, 0:nq, :], in_=scr[nm]["g"][q0 * 128:(q0 + nq) * 128, hc].rearrange("(t p) c -> p t c", p=128)),
                            reads=[dbuf(nm, "g", q0 + i) for i in range(nq)], writes=[g_b], dma=True)
                        qt, qt_b = QT.get()
                        bk = tp_i[0] % 2
                        tp_i[0] += 1
                        ptv = PST[bk]
                        blocks = [ql[:, tt, m * 128:(m + 1) * 128] for tt in range(nq) for m in range(2)]
                        for j, src in enumerate(blocks):
                            emit("pe", lambda e, j=j, src=src, ptv=ptv: e.transpose(ptv[:, j, :], src, ident[:]),
                                 reads=[ql_b, *CONST], writes=[PST_b[bk]], signal=(j == len(blocks) - 1))
                        for m in range(2):
                            emit("act", lambda e, ptv=ptv, nq=nq, m=m, qt=qt: e.activation(
                                out=qt[:, m, 0:nq * 128].rearrange("p (t c) -> p t c", c=128),
                                in_=ptv[:, 0:2 * nq, :].rearrange("p (t m) c -> p t m c", m=2)[:, :, m, :], func=AF.Copy),
                                reads=[PST_b[bk]], writes=[qt_b])
                        ot, ot_b = o0.get()
                        for m in range(2):
                            def emit_qk(kt, m=m, qt=qt, qt_b=qt_b, nq=nq):
                                bs = 4 + (sbank[0] % 2)
                                sbank[0] += 1
                                emit("pe", lambda e, bs=bs, m=m, kt=kt, qt=qt, nq=nq: e.matmul(
                                    PSB[bs][:, 0:nq * 128], KT[:, m, kt * 128:(kt + 1) * 128], qt[:, m, 0:nq * 128],
                                    start=True, stop=True), reads=[KT_b, qt_b], writes=[PSB_b[bs]])
                                pt, pt_b = Pt.get()
                                bias = ptab_s[:, 2:3] if kt == 0 else zero_c[:, 0:1]
                                emit("act", lambda e, pt=pt, bs=bs, nq=nq, bias=bias: e.activation(
                                    out=pt[:, 0:nq * 128], in_=PSB[bs][:, 0:nq * 128], func=AF.Exp, bias=bias, scale=scale),
                                    reads=[PSB_b[bs], *CONST], writes=[pt_b])
                                return pt, pt_b

                            pend = emit_qk(0)
                            for kt in range(NT):
                                nxt = emit_qk(kt + 1) if kt + 1 < NT else None
                                pt, pt_b = pend
                                for qi in range(nq):
                                    emit("pe", lambda e, qi=qi, pt=pt, kt=kt: e.matmul(
                                        PSB[qi][:, 0:257], pt[:, qi * 128:(qi + 1) * 128], Vx[:, kt, :],
                                        start=(kt == 0), stop=(kt == NT - 1)),
                                        reads=[pt_b, Vx_b[kt // 8]], writes=[PSB_b[qi]], signal=(kt == NT - 1))
                                pend = nxt
                            for qi in range(nq):
                                stt, st_b = stat.get()
                                emit("dve", lambda e, stt=stt, qi=qi: e.reciprocal(out=stt[:, 0:1], in_=PSB[qi][:, 256:257]),
                                     reads=[PSB_b[qi]], writes=[st_b])
                                if m == 0:
                                    emit("dve", lambda e, ot=ot, qi=qi, stt=stt: e.tensor_scalar(
                                        out=ot[:, qi, :], in0=PSB[qi][:, 0:256], scalar1=stt[:, 0:1], scalar2=None, op0=ALU.mult),
                                        reads=[PSB_b[qi], st_b], writes=[ot_b])
                                else:
                                    emit("dve", lambda e, stt=stt: e.tensor_tensor(
                                        out=stt[:, 1:2], in0=stt[:, 0:1], in1=lam_t[:, 4:5], op=ALU.mult),
                                        reads=[st_b, *CONST], writes=[st_b])
                                    emit("dve", lambda e, ot=ot, qi=qi, stt=stt: e.scalar_tensor_tensor(
                                        out=ot[:, qi, :], in0=PSB[qi][:, 0:256], scalar=stt[:, 1:2], in1=ot[:, qi, :],
                                        op0=ALU.mult, op1=ALU.add),
                                        reads=[PSB_b[qi], st_b, ot_b], writes=[ot_b])
                        for qi in range(nq):
                            jk, jk_b = junk.get()
                            stt, st_b = rms_rstd(stat, jk[:], jk_b, ot[:, qi, :], ot_b, 256, 1e-5)
                            tm, tm_b = tmpo.get()
                            emit("dve", lambda e, tm=tm, ot=ot, qi=qi, stt=stt: e.scalar_tensor_tensor(
                                out=tm[:], in0=ot[:, qi, :], scalar=stt[:, 3:4], in1=subln_s[:], op0=ALU.mult, op1=ALU.mult),
                                reads=[ot_b, st_b, *CONST], writes=[tm_b])
                            o_, o_b = ob.get()
                            emit("dve", lambda e, o_=o_, tm=tm, g_=g_, qi=qi: e.scalar_tensor_tensor(
                                out=o_[:], in0=tm[:], scalar=float(1.0 - lam_init), in1=g_[:, qi, :], op0=ALU.mult, op1=ALU.mult),
                                reads=[tm_b, g_b], writes=[o_b])
                            t = q0 + qi
                            dst = scr[nm]["o"][t * 128:(t + 1) * 128, hc]
                            emit("pool", lambda e, dst=dst, o_=o_: e.dma_start(out=dst, in_=o_[:]),
                                 reads=[o_b], writes=[dbuf(nm, "o", t)], dma=True)
            return body

        def ph_outproj(nm, layer):
            S = dict(seqs)[nm]
            NT = S // 128 + 1
            wout = rwout if layer == 0 else dwout
            KO = 2 * KC

            def body(st):
                AT2 = tsb(st, "AT2", [128, KO, TB_OUT * 128], BF16)
                AT_b = [Buf() for _ in range(TB_OUT)]
                WT = TPool(st, "WT", [128, KC, 512], BF16, 2)
                orow = TPool(st, "orow", [128, 2 * D], BF16, 2)
                stg = TPool(st, "stg", [128, 512], F32, 3)
                for blk in range(1 if layer == 1 else 0, NT, TB_OUT):
                    tiles = list(range(blk, min(NT, blk + TB_OUT)))
                    for ti, t in enumerate(tiles):
                        ot, ot_b = orow.get()
                        emit("sp", lambda e, ot=ot, t=t: e.dma_start(out=ot[:], in_=scr[nm]["o"][t * 128:(t + 1) * 128, 0:2 * D]),
                             reads=[dbuf(nm, "o", t)], writes=[ot_b], dma=True)
                        transposes([ot[:, c * 128:(c + 1) * 128] for c in range(KO)], [ot_b],
                                   lambda i0, n, ti=ti: AT2[:, i0:i0 + n, ti * 128:(ti + 1) * 128], [AT_b[ti]])
                    for cb in range(D // 512):
                        c0 = cb * 512
                        for half in range(2):
                            wt, wb = WT.get()
                            nparts = max(1, KC // 8)
                            per = KC // nparts
                            toks = {}
                            for part in range(nparts):
                                k0 = half * KC * 128 + part * per * 128
                                src = wout[k0:k0 + per * 128, c0:c0 + 512].rearrange("(a p) c -> p a c", p=128)
                                tmpb = Buf()
                                if part == 0:
                                    tmpb.w, tmpb.r = wb.w, wb.r
                                tok = emit("pool", lambda e, wt=wt, src=src, part=part, per=per: e.dma_start(
                                    out=wt[:, part * per:(part + 1) * per, :], in_=src), writes=[tmpb], dma=True)
                                toks[tok[0]] = (tok[1], tok[2])
                            wb.w, wb.r = toks, {}
                            for ti, t in enumerate(tiles):
                                for kc in range(KC):
                                    first = (half == 0 and kc == 0)
                                    last = (half == 1 and kc == KC - 1)
                                    emit("pe", lambda e, ti=ti, kc=kc, wt=wt, half=half, first=first, last=last: e.matmul(
                                        PSB[ti][:, :], AT2[:, half * KC + kc, ti * 128:(ti + 1) * 128], wt[:, kc, :],
                                        start=first, stop=last),
                                        reads=[AT_b[ti], wb], writes=[PSB_b[ti]], signal=(kc == KC - 1))
                        for ti, t in enumerate(tiles):
                            sg, sg_b = stg.get()
                            emit("act", lambda e, sg=sg, ti=ti: e.activation(out=sg[:], in_=PSB[ti][:, :], func=AF.Copy),
                                 reads=[PSB_b[ti]], writes=[sg_b])
                            dst = scr[nm]["m"][t * 128:(t + 1) * 128, c0:c0 + 512]
                            emit("pool", lambda e, dst=dst, sg=sg: e.dma_start(out=dst, in_=sg[:]),
                                 reads=[sg_b], writes=[dbuf(nm, "m", t)], dma=True)
            return body

        def ph_norm(nm, layer):
            S = dict(seqs)[nm]
            NT = S // 128 + 1

            def body(st):
                xrow = TPool(st, "xrow", [128, D], F32, 2)
                mrow = TPool(st, "mrow", [128, D], F32, 2)
                jrow = TPool(st, "jrow", [128, D], BF16, 2)
                stat = TPool(st, "stat", [128, 8], F32, 4)
                load_gain(post_n[layer:layer + 1, :])
                for t in range(NT):
                    if layer == 1 and t == 0:
                        continue
                    xt, xb = xrow.get()
                    load_x(nm, layer, t, xt, xb)
                    mt, mb = mrow.get()
                    emit("sp", lambda e, mt=mt, t=t: e.dma_start(out=mt[:], in_=scr[nm]["m"][t * 128:(t + 1) * 128, :]),
                         reads=[dbuf(nm, "m", t)], writes=[mb], dma=True)
                    jt, jb = jrow.get()
                    stt, st_b = rms_rstd(stat, jt[:], jb, mt[:], mb, D, 1e-6)
                    emit("dve", lambda e, mt=mt, stt=stt: e.scalar_tensor_tensor(
                        out=mt[:], in0=mt[:], scalar=stt[:, 3:4], in1=gain_t[:], op0=ALU.mult, op1=ALU.mult),
                        reads=[mb, st_b, gain_b], writes=[mb])
                    emit("dve", lambda e, mt=mt, xt=xt: e.tensor_tensor(out=xt[:], in0=xt[:], in1=mt[:], op=ALU.add),
                         reads=[mb, xb], writes=[xb])
                    if layer == 0:
                        dst = scr[nm]["x1"][t * 128:(t + 1) * 128, :]
                        wbuf = dbuf(nm, "x1", t)
                    else:
                        dst = yout[nm][(t - 1) * 128:t * 128, :]
                        wbuf = dbuf(nm, "y", t)
                    emit("pool", lambda e, dst=dst, xt=xt: e.dma_start(out=dst, in_=xt[:]),
                         reads=[xb], writes=[wbuf], dma=True)
            return body

        run_phase(ph_const)
        stop_after = cfg.get("stop_after")
        for nm, S in seqs:
            plist = [("in0", ph_inproj(nm, 0)), ("ret", ph_ret(nm)), ("out0", ph_outproj(nm, 0)), ("norm0", ph_norm(nm, 0)),
                     ("in1", ph_inproj(nm, 1)), ("att", ph_att(nm)), ("out1", ph_outproj(nm, 1)), ("norm1", ph_norm(nm, 1))]
            for pname, ph in plist:
                run_phase(ph)
                if stop_after == pname:
                    break
    return nc


def _rot_table(L, rot_dim, theta):
    half = rot_dim // 2
    inv_freq = np.power(np.float32(theta), -np.arange(half, dtype=np.float32) * np.float32(2.0) / np.float32(rot_dim)).astype(np.float32)
    pos = np.arange(L, dtype=np.float32)
    ang = (pos[:, None] * inv_freq[None, :]).astype(np.float32)
    return np.cos(ang).astype(np.float32), np.sin(ang).astype(np.float32)


def _const_tables(cfg):
    Lmax = max(cfg["S_S"], cfg["S_P"]) + 128
    tabs = {}
    for nm, rd, th in (("rtab", 256, RET_THETA), ("dtab", 32, ROPE_THETA)):
        c, s = _rot_table(Lmax, rd, th)
        t = np.zeros((Lmax, 2, rd // 2), np.float32)
        t[:, 0, :] = 1.0
        t[PADR:, 0, :] = c[:Lmax - PADR]
        t[PADR:, 1, :] = s[:Lmax - PADR]
        tabs[nm] = t
    j = np.arange(128, dtype=np.float32)[:, None]
    i = np.arange(128, dtype=np.float32)[None, :]
    ctab = np.zeros((128, 6, 128), np.float32)
    ctab[:, 0, :] = np.maximum(i - j, 0)
    ctab[:, 1, :] = (i >= j)
    ctab[:, 2, :] = np.maximum(j - i, 0)
    ctab[:, 3, :] = (j > i)
    ctab[:, 4, :] = i + 1 + 0 * j
    ctab[:, 5, :] = 128 - i + 0 * j
    ptab = np.zeros((128, 4), np.float32)
    ptab[:, 0] = 127 - np.arange(128)
    ptab[:, 1] = np.arange(128)
    ptab[:PADR, 2] = -30000.0
    tabs["ctab"] = ctab
    tabs["ptab"] = ptab
    tabs["ident"] = np.eye(128, dtype=np.float32)
    return tabs


_PROG_CACHE = {}


def run_cfg(cfg, inputs, debug=False):
    key = (tuple(sorted(cfg.items())), debug)
    if key not in _PROG_CACHE:
        _PROG_CACHE[key] = build_program(cfg, debug=debug)
    nc = _PROG_CACHE[key]
    f = lambda a: np.ascontiguousarray(np.asarray(a, dtype=np.float32))
    xp, xs = f(inputs["x_prompt"]), f(inputs["x_sample"])
    shared = dict(
        meta=f(inputs["meta_tokens"]), pre_norm=f(inputs["pre_norm"]), post_norm=f(inputs["post_norm"]),
        ret_w_in=f(inputs["ret_w_in"][0]), ret_w_out=f(inputs["ret_w_out"][0]),
        ret_decay=f(np.concatenate([np.asarray(inputs["ret_decay_fwd"]), np.asarray(inputs["ret_decay_bwd"])], 0)),
        diff_w_in=f(inputs["diff_w_in"][0]), diff_w_out=f(inputs["diff_w_out"][0]),
        diff_lam=f(np.concatenate([np.asarray(inputs[k]) for k in
                                   ("diff_lambda_q1", "diff_lambda_k1", "diff_lambda_q2", "diff_lambda_k2")], 0)),
        diff_subln=f(inputs["diff_subln"]),
    )
    shared.update(_const_tables(cfg))
    in_maps = []
    for c in range(8):
        m = dict(shared)
        m["x_p"] = xp[c]
        m["x_s"] = xs[c % xs.shape[0]]
        in_maps.append(m)
    res = run_bass_kernel_spmd(nc, in_maps, core_ids=list(range(8)))
    y_p = np.stack([res.results[c]["y_p"] for c in range(8)], 0)
    y_s = np.stack([res.results[c]["y_s"] for c in range(xs.shape[0])], 0)
    return (y_p, y_s), res


def kernel(**inputs):
    (y_p, y_s), _ = run_cfg(FULL, inputs)
    return (y_p.astype(np.float32), y_s.astype(np.float32))
```
